# Optimizing a Trainium2 kernel written in Bass

```python
import math
import jax, jax.numpy as jnp
from jax import lax
import numpy as np

D_MODEL = 1024
BATCH = 8
SEQ = 8192
DEPTH = 1

GRID_W = 64
CTX_LEN = 256
N_MOD = 6
DIFF_HEADS = 4
DIFF_DH = 64
DIFF_VD = 2 * DIFF_DH
ATTN_WIDTH = DIFF_HEADS * DIFF_VD
HY_WIDTH = D_MODEL - ATTN_WIDTH
HY_ORDER = 2
HY_DIRS = 2
HY_EMB_DIM = 33
HY_BANDS = (HY_EMB_DIM - 1) // 2
HY_FILTER_ORDER = 64
HY_DECAY_TARGET = 1e-2
HY_FAST_DECAY_PCT = 0.3
HY_SLOW_DECAY_PCT = 1.5
Q_COLS = DIFF_HEADS * 2 * DIFF_DH
K_COLS = Q_COLS
V_COLS = ATTN_WIDTH
HY_COLS = (HY_ORDER + 1) * HY_WIDTH
IN_COLS = Q_COLS + K_COLS + V_COLS + HY_COLS
MIX_WIDTH = ATTN_WIDTH + HY_WIDTH
D_FF = 2816
SHORT_CONV = 3
ROPE_BASE = 10000.0
QBLOCK = 128
EPS = 1e-6

kernel_name = "hybrid_diffattn_hyena_dit_layer"

F32 = jnp.float32


def rmsnorm(x, g):
    xf = x.astype(F32)
    y = xf * lax.rsqrt(jnp.mean(xf * xf, axis=-1, keepdims=True) + EPS) * g.astype(F32)
    return y.astype(x.dtype)


def modulate(h, shift, scale):
    return h * (1.0 + scale) + shift


def split_mod(mod):
    m = mod.reshape(mod.shape[:-1] + (1, mod.shape[-1]))
    return jnp.split(m, N_MOD, axis=-1)


def short_conv(x, w, b):
    L = x.shape[1]
    pad = SHORT_CONV // 2
    xp = jnp.pad(x, ((0, 0), (pad, pad), (0, 0)))
    return sum(xp[:, i:i + L] * w[i] for i in range(SHORT_CONV)) + b


def heads_qk(t):
    B, L, _ = t.shape
    return t.reshape(B, L, DIFF_HEADS, 2, DIFF_DH).transpose(0, 2, 3, 1, 4)


def heads_v(t):
    B, L, _ = t.shape
    return t.reshape(B, L, DIFF_HEADS, DIFF_VD).transpose(0, 2, 1, 3)


def axial_rope(t, row, col):
    half = DIFF_DH // 2
    nf = half // 2
    inv = ROPE_BASE ** (-jnp.arange(nf, dtype=F32) / nf)

    def rot(part, pos):
        ang = pos.astype(F32)[:, None] * inv
        cos, sin = jnp.cos(ang), jnp.sin(ang)
        p1 = part[..., :nf].astype(F32)
        p2 = part[..., nf:].astype(F32)
        return jnp.concatenate([p1 * cos - p2 * sin, p2 * cos + p1 * sin], axis=-1)

    out = jnp.concatenate([rot(t[..., :half], row), rot(t[..., half:], col)], axis=-1)
    return out.astype(t.dtype)


def diff_maps(q, k, v, lam):
    s = jnp.einsum("bhmqd,bhmkd->bhmqk", q, k, preferred_element_type=F32) * (DIFF_DH ** -0.5)
    a = jax.nn.softmax(s, axis=-1)
    w = a[:, :, 0] - lam * a[:, :, 1]
    return jnp.einsum("bhqk,bhkd->bhqd", w.astype(v.dtype), v)


def diff_attn_blocked(q, k, v, lam):
    B, H, M, S, DH = q.shape
    nb = S // QBLOCK
    qb = jnp.moveaxis(q.reshape(B, H, M, nb, QBLOCK, DH), 3, 0)
    o = lax.map(lambda qi: diff_maps(qi, k, v, lam), qb)
    return jnp.moveaxis(o, 0, 2).reshape(B, H, S, DIFF_VD)


def diff_out(o, subln_g, lam_init):
    o = rmsnorm(o, subln_g) * (1.0 - lam_init)
    B, H, L, VD = o.shape
    return o.transpose(0, 2, 1, 3).reshape(B, L, H * VD)


def hyena_filters(L, p):
    t = jnp.linspace(0.0, 1.0, L, dtype=F32)[:, None]
    w = 2.0 * math.pi * jnp.arange(L, dtype=F32)[:, None] / L
    f = jnp.linspace(1e-4, HY_BANDS - 1, HY_BANDS, dtype=F32)[None, :]
    z = jnp.concatenate([t, jnp.cos(f * w), -jnp.sin(f * w)], axis=-1)
    freq = p["hy_freq"].astype(F32)
    h = jnp.sin(freq * (z @ p["hy_w1"].astype(F32) + p["hy_b1"].astype(F32)))
    h = jnp.sin(freq * (h @ p["hy_w2"].astype(F32) + p["hy_b2"].astype(F32)))
    h = jnp.sin(freq * (h @ p["hy_w3"].astype(F32) + p["hy_b3"].astype(F32)))
    h = h @ p["hy_w4"].astype(F32)
    min_decay = math.log(HY_DECAY_TARGET) / HY_FAST_DECAY_PCT
    max_decay = math.log(HY_DECAY_TARGET) / HY_SLOW_DECAY_PCT
    deltas = jnp.abs(jnp.linspace(min_decay, max_decay, h.shape[-1], dtype=F32))
    h = h * jnp.exp(-t * deltas)
    h = h.reshape(L, HY_ORDER, HY_DIRS, HY_WIDTH)
    k_full = jnp.concatenate(
        [h[:, :, 0], jnp.zeros((1, HY_ORDER, HY_WIDTH), F32), h[:0:-1, :, 1]], axis=0)
    return jnp.fft.rfft(k_full, axis=0)


def long_conv(z, k_f):
    L = z.shape[1]
    zf = jnp.fft.rfft(z.astype(F32), n=2 * L, axis=1)
    y = jnp.fft.irfft(zf * k_f[None], n=2 * L, axis=1)[:, :L]
    return y.astype(z.dtype)


def hyena_mixer(proj, p, k_f):
    u = short_conv(proj, p["hy_conv_w"], p["hy_conv_b"])
    v, x1, x2 = jnp.split(u, HY_ORDER + 1, axis=-1)
    z = v
    for n, gate in enumerate((x1, x2)):
        z = gate * (long_conv(z, k_f[:, n]) + p["hy_skip"][n] * z)
    return rmsnorm(z, p["hy_norm"])


def conv_ffn(h, p):
    a, b = jnp.split(h @ p["ffn_w_up"], 2, axis=-1)
    a = short_conv(a, p["ffn_conv_w"], p["ffn_conv_b"])
    return (jax.nn.gelu(a, approximate=False) * b) @ p["ffn_w_down"]


def trunk_layer(x, ctx, mod_x, mod_c, p, layer_idx, update_ctx):
    L = x.shape[1]
    rows = L // GRID_W
    row = jnp.repeat(jnp.arange(rows, dtype=jnp.int32), GRID_W)
    col = jnp.tile(jnp.arange(GRID_W, dtype=jnp.int32), rows)
    sh_a, sc_a, g_a, sh_f, sc_f, g_f = split_mod(mod_x)
    csh_a, csc_a, cg_a, csh_f, csc_f, cg_f = split_mod(mod_c)

    lam_init = 0.8 - 0.6 * math.exp(-0.3 * layer_idx)
    lam = (jnp.exp(jnp.sum(p["lam_q1"].astype(F32) * p["lam_k1"].astype(F32)))
           - jnp.exp(jnp.sum(p["lam_q2"].astype(F32) * p["lam_k2"].astype(F32)))
           + lam_init)
    cuts = [Q_COLS, Q_COLS + K_COLS, Q_COLS + K_COLS + V_COLS]
    w_q, w_k, w_v, w_hy = jnp.split(p["w_in"], cuts, axis=1)

    hx = modulate(rmsnorm(x, p["norm_mix"]), sh_a, sc_a)
    hc = modulate(rmsnorm(ctx, p["norm_mix"]), csh_a, csc_a)
    q_x, k_x, v_x, hy_x = jnp.split(hx @ p["w_in"], cuts, axis=-1)
    k_c = heads_qk(hc @ w_k)
    v_c = heads_v(hc @ w_v)
    q_x = axial_rope(heads_qk(q_x), row, col)
    k_x = axial_rope(heads_qk(k_x), row, col)
    k_all = jnp.concatenate([k_c, k_x], axis=3)
    v_all = jnp.concatenate([v_c, heads_v(v_x)], axis=2)
    attn_x = diff_out(diff_attn_blocked(q_x, k_all, v_all, lam), p["subln"], lam_init)
    hyo_x = hyena_mixer(hy_x, p, hyena_filters(L, p))
    x = x + g_a * (jnp.concatenate([attn_x, hyo_x], axis=-1) @ p["w_out"])

    x = x + g_f * conv_ffn(modulate(rmsnorm(x, p["norm_ffn"]), sh_f, sc_f), p)

    if update_ctx:
        attn_c = diff_out(diff_maps(heads_qk(hc @ w_q), k_c, v_c, lam), p["subln"], lam_init)
        hyo_c = hyena_mixer(hc @ w_hy, p, hyena_filters(ctx.shape[1], p))
        ctx = ctx + cg_a * (jnp.concatenate([attn_c, hyo_c], axis=-1) @ p["w_out"])
        ctx = ctx + cg_f * conv_ffn(modulate(rmsnorm(ctx, p["norm_ffn"]), csh_f, csc_f), p)
    return x, ctx


def setup_inputs(seed: int = 0) -> dict:
    key = jax.random.key(seed)
    ks = jax.random.split(key, 32)

    def nrm(k, shape, scale):
        return jax.random.normal(k, shape, F32) * scale

    def gain(k, shape):
        return 1.0 + nrm(k, shape, 0.02)

    return {
        "x": nrm(ks[0], (BATCH, SEQ, D_MODEL), 1.0),
        "c": nrm(ks[1], (BATCH, D_MODEL), 1.0),
        "ctx": nrm(ks[2], (BATCH, CTX_LEN, D_MODEL), 1.0),
        "c_ctx": nrm(ks[3], (D_MODEL,), 1.0),
        "w_mod": nrm(ks[4], (DEPTH, D_MODEL, N_MOD * D_MODEL), 0.3 * D_MODEL ** -0.5),
        "b_mod": nrm(ks[5], (DEPTH, N_MOD * D_MODEL), 0.01),
        "norm_mix": gain(ks[6], (DEPTH, D_MODEL)),
        "norm_ffn": gain(ks[7], (DEPTH, D_MODEL)),
        "w_in": nrm(ks[8], (DEPTH, D_MODEL, IN_COLS), D_MODEL ** -0.5),
        "lam_q1": nrm(ks[9], (DEPTH, DIFF_DH), 0.1),
        "lam_k1": nrm(ks[10], (DEPTH, DIFF_DH), 0.1),
        "lam_q2": nrm(ks[11], (DEPTH, DIFF_DH), 0.1),
        "lam_k2": nrm(ks[12], (DEPTH, DIFF_DH), 0.1),
        "subln": gain(ks[13], (DEPTH, DIFF_VD)),
        "hy_conv_w": nrm(ks[14], (DEPTH, SHORT_CONV, HY_COLS), SHORT_CONV ** -0.5),
        "hy_conv_b": nrm(ks[15], (DEPTH, HY_COLS), 0.01),
        "hy_w1": nrm(ks[16], (DEPTH, HY_EMB_DIM, HY_FILTER_ORDER), HY_EMB_DIM ** -0.5),
        "hy_b1": nrm(ks[17], (DEPTH, HY_FILTER_ORDER), 0.1),
        "hy_w2": nrm(ks[18], (DEPTH, HY_FILTER_ORDER, HY_FILTER_ORDER), HY_FILTER_ORDER ** -0.5),
        "hy_b2": nrm(ks[19], (DEPTH, HY_FILTER_ORDER), 0.1),
        "hy_w3": nrm(ks[20], (DEPTH, HY_FILTER_ORDER, HY_FILTER_ORDER), HY_FILTER_ORDER ** -0.5),
        "hy_b3": nrm(ks[21], (DEPTH, HY_FILTER_ORDER), 0.1),
        "hy_w4": nrm(ks[22], (DEPTH, HY_FILTER_ORDER, HY_ORDER * HY_DIRS * HY_WIDTH), 0.1 * HY_FILTER_ORDER ** -0.5),
        "hy_freq": gain(ks[23], (DEPTH, HY_FILTER_ORDER)),
        "hy_skip": nrm(ks[24], (DEPTH, HY_ORDER, HY_WIDTH), 0.5),
        "hy_norm": gain(ks[25], (DEPTH, HY_WIDTH)),
        "w_out": nrm(ks[26], (DEPTH, MIX_WIDTH, D_MODEL), MIX_WIDTH ** -0.5),
        "ffn_w_up": nrm(ks[27], (DEPTH, D_MODEL, 2 * D_FF), D_MODEL ** -0.5),
        "ffn_conv_w": nrm(ks[28], (DEPTH, SHORT_CONV, D_FF), SHORT_CONV ** -0.5),
        "ffn_conv_b": nrm(ks[29], (DEPTH, D_FF), 0.01),
        "ffn_w_down": nrm(ks[30], (DEPTH, D_FF, D_MODEL), D_FF ** -0.5),
        "final_norm": gain(ks[31], (D_MODEL,)),
    }


def reference(x, c, ctx, c_ctx, w_mod, b_mod, norm_mix, norm_ffn, w_in,
              lam_q1, lam_k1, lam_q2, lam_k2, subln,
              hy_conv_w, hy_conv_b, hy_w1, hy_b1, hy_w2, hy_b2, hy_w3, hy_b3, hy_w4,
              hy_freq, hy_skip, hy_norm, w_out,
              ffn_w_up, ffn_conv_w, ffn_conv_b, ffn_w_down, final_norm):
    x_lat, x_ctx = x, ctx
    silu_c = jax.nn.silu(c)
    silu_cc = jax.nn.silu(c_ctx)
    for l in range(DEPTH):
        p = {
            "norm_mix": norm_mix[l], "norm_ffn": norm_ffn[l], "w_in": w_in[l],
            "lam_q1": lam_q1[l], "lam_k1": lam_k1[l], "lam_q2": lam_q2[l], "lam_k2": lam_k2[l],
            "subln": subln[l], "hy_conv_w": hy_conv_w[l], "hy_conv_b": hy_conv_b[l],
            "hy_w1": hy_w1[l], "hy_b1": hy_b1[l], "hy_w2": hy_w2[l], "hy_b2": hy_b2[l],
            "hy_w3": hy_w3[l], "hy_b3": hy_b3[l], "hy_w4": hy_w4[l], "hy_freq": hy_freq[l],
            "hy_skip": hy_skip[l], "hy_norm": hy_norm[l], "w_out": w_out[l],
            "ffn_w_up": ffn_w_up[l], "ffn_conv_w": ffn_conv_w[l], "ffn_conv_b": ffn_conv_b[l],
            "ffn_w_down": ffn_w_down[l],
        }
        mod_x = silu_c @ w_mod[l] + b_mod[l]
        mod_c = silu_cc @ w_mod[l] + b_mod[l]
        x_lat, x_ctx = trunk_layer(x_lat, x_ctx, mod_x, mod_c, p, l, l < DEPTH - 1)
    return rmsnorm(x_lat, final_norm)
```

```python
import math
from contextlib import ExitStack
import numpy as np
import ml_dtypes
import concourse.bass as bass
import concourse.mybir as mybir
from concourse.bass_utils import run_bass_kernel_spmd

F32 = mybir.dt.float32
BF16 = mybir.dt.bfloat16
AF = mybir.ActivationFunctionType
ALU = mybir.AluOpType
AX = mybir.AxisListType

D = 1024
L = 8192
CTX = 256
NKEY = CTX + L
DFF = 2816
EPS = 1e-6
LAM_INIT = 0.2
NCH = 16
TWO_PI = 2.0 * math.pi
NB = 32
NFFT = 16384
PI_LO = 3.1415925


class Res:
    __slots__ = ("name", "w", "r")

    def __init__(self, name):
        self.name = name
        self.w = None
        self.r = {}


class Prog:
    EPOCH = 24000
    NRING = 12

    def __init__(self, nc, es):
        self.nc, self.es = nc, es
        self.eng = {"pe": nc.tensor, "act": nc.scalar, "dve": nc.vector, "pool": nc.gpsimd, "sp": nc.sync}
        self.nsem = 0
        self.ops = {k: [] for k in self.eng}
        self.cnt = {k: 0 for k in self.eng}
        self.cur = {k: self.newsem(k) for k in self.eng}
        self.pesems = {self.cur["pe"]}
        self.waited = {k: {} for k in self.eng}
        self.ring = {q: [[self.newsem(q + "r"), 0] for _ in range(self.NRING)] for q in ("sp", "pool", "act")}
        self.ringpos = {q: 0 for q in self.ring}
        self.nres = 0

    def newsem(self, nm):
        self.nsem += 1
        return self.es.enter_context(self.nc.semaphore(f"s_{nm}_{self.nsem}"))

    def res(self, name=None):
        self.nres += 1
        return Res(name or f"r{self.nres}")

    def sb(self, name, shape, dt, glob=False):
        return (self.es if glob else self.pes).enter_context(self.nc.sbuf_tensor("t_" + name, list(shape), dt))

    def ps(self, name, shape, dt, glob=False):
        return (self.es if glob else self.pes).enter_context(self.nc.psum_tensor("p_" + name, list(shape), dt))

    def barrier(self):
        alltok = []
        for q in self.ring:
            for sem, val in self.ring[q]:
                if val > 0:
                    alltok.append((sem, val))
        for e in ("pe", "act", "dve", "pool"):
            if self.cnt[e] > 0:
                alltok.append((self.cur[e], self.cnt[e]))
        for e in self.eng:
            waits = []
            for s, v in alltok:
                if e != "sp" and s is self.cur[e]:
                    continue
                if self.waited[e].get(s, 0) >= v:
                    continue
                self.waited[e][s] = v
                waits.append((s, v))
            self.ops[e].append((waits, None, None, 0))

    def temp_scope(self):
        prog = self

        class _S:
            def __enter__(self_):
                self_.old = prog.pes
                self_.es = ExitStack()
                self_.es.__enter__()
                prog.pes = self_.es
                return self_

            def __exit__(self_, *a):
                prog.pes = self_.old
                self_.es.__exit__(None, None, None)
                prog.barrier()
                return False
        return _S()

    def end_phase(self):
        self.barrier()
        self.emit()
        self.ops = {k: [] for k in self.eng}

    def _collect(self, e, reads, writes, extra=()):
        toks = {}

        def need(s, v):
            if toks.get(s, 0) < v:
                toks[s] = v

        for r in reads:
            if r.w is not None:
                need(*r.w)
        for w in writes:
            if w.w is not None:
                need(*w.w)
            for s, v in w.r.items():
                need(s, v)
        for s, v in extra:
            need(s, v)
        waits = []
        for s, v in toks.items():
            if e == "pe" and s in self.pesems:
                continue
            if self.waited[e].get(s, 0) >= v:
                continue
            self.waited[e][s] = v
            waits.append((s, v))
        return waits

    def _commit(self, tok, reads, writes):
        s, v = tok
        for r in reads:
            if r.r.get(s, 0) < v:
                r.r[s] = v
        for w in writes:
            w.w = tok
            w.r = {}

    def op(self, e, fn, reads=(), writes=()):
        waits = self._collect(e, reads, writes)
        if self.cnt[e] >= self.EPOCH:
            self.cur[e] = self.newsem(e)
            self.cnt[e] = 0
            if e == "pe":
                self.pesems.add(self.cur[e])
        self.cnt[e] += 1
        tok = (self.cur[e], self.cnt[e])
        self.ops[e].append((waits, fn, tok[0], 1))
        self._commit(tok, reads, writes)
        return tok

    def dma(self, q, out, in_, reads=(), writes=(), **kw):
        slot = self.ring[q][self.ringpos[q] % self.NRING]
        self.ringpos[q] += 1
        sem, val = slot
        extra = [(sem, val)] if val > 0 else []
        waits = self._collect(q, reads, writes, extra)
        slot[1] = val + 16
        tok = (sem, val + 16)
        self.ops[q].append((waits, (lambda e, o=out, i=in_, k=kw: e.dma_start(out=o, in_=i, **k)), sem, 16))
        self._commit(tok, reads, writes)
        return tok

    def emit(self):
        with self.nc.Block() as block:
            def mk(e):
                def body(eng):
                    for waits, fn, sem, inc in self.ops[e]:
                        for s, v in waits:
                            eng.wait_ge(s, v)
                        if fn is not None:
                            fn(eng).then_inc(sem, inc)
                return body
            block.tensor(mk("pe"))
            block.scalar(mk("act"))
            block.vector(mk("dve"))
            block.gpsimd(mk("pool"))
            block.sync(mk("sp"))


class Rot:
    def __init__(self, P, name, shape, dt, n, psum=False):
        self.items = []
        for i in range(n):
            t = (P.ps if psum else P.sb)(f"{name}{i}", shape, dt)
            self.items.append((t, P.res(f"{name}{i}")))
        self.i = 0

    def next(self):
        it = self.items[self.i % len(self.items)]
        self.i += 1
        return it


def rope_tables():
    nf = 16
    inv = (10000.0 ** (-np.arange(nf, dtype=np.float32) / nf)).astype(np.float32)
    t = np.arange(L)
    row = (t // 64).astype(np.float32)
    col = (t % 64).astype(np.float32)
    C = np.zeros((128, L), np.float32)
    S = np.zeros((128, L), np.float32)
    for dp in range(128):
        d = dp % 64
        if d < 32:
            pos, jj, sgn = row, d % 16, (-1.0 if d < 16 else 1.0)
        else:
            pos, jj, sgn = col, (d - 32) % 16, (-1.0 if (d - 32) < 16 else 1.0)
        ang = (pos * inv[jj]).astype(np.float32)
        C[dp] = np.cos(ang)
        S[dp] = sgn * np.sin(ang)
    return C, S


class IO:
    pass


def declare_io(nc, dbg_out=(), dbg_in=()):
    io = IO()

    def inp(name, shape, dt=F32):
        t = nc.dram_tensor(name, list(shape), dt, kind="ExternalInput").ap()
        setattr(io, name, t)
        return t

    def scratch(name, shape, dt):
        kind = "Internal"
        if name in dbg_out:
            kind = "ExternalOutput"
        if name in dbg_in:
            kind = "ExternalInput"
        t = nc.dram_tensor(name, list(shape), dt, kind=kind).ap()
        setattr(io, name, t)
        return t

    inp("x", [L, D]); inp("c", [D]); inp("ctx", [CTX, D]); inp("c_ctx", [D])
    inp("w_mod", [D, 6 * D]); inp("b_mod", [6 * D]); inp("norm_mix", [D]); inp("norm_ffn", [D])
    inp("w_in", [D, 3072])
    inp("lam_q1", [64]); inp("lam_k1", [64]); inp("lam_q2", [64]); inp("lam_k2", [64]); inp("subln", [128])
    inp("hy_conv_w", [3, 1536]); inp("hy_conv_b", [1536])
    inp("hy_w1", [33, 64]); inp("hy_b1", [64]); inp("hy_w2", [64, 64]); inp("hy_b2", [64])
    inp("hy_w3", [64, 64]); inp("hy_b3", [64]); inp("hy_w4", [64, 2048]); inp("hy_freq", [64])
    inp("hy_skip", [2, 512]); inp("hy_norm", [512]); inp("w_out", [D, D])
    inp("ffn_w_up", [D, 2 * DFF]); inp("ffn_conv_w", [3, DFF]); inp("ffn_conv_b", [DFF]); inp("ffn_w_down", [DFF, D])
    inp("final_norm", [D])
    inp("ident_f", [128, 128]); inp("ident_b", [128, 128], BF16)
    inp("rope_c", [128, L]); inp("rope_s", [128, L])
    inp("fft_d1", [128, 260], BF16); inp("fft_g", [128, 65 * 2 * 128], BF16)
    inp("fft_e1", [128, 256], BF16); inp("fft_e2", [128, 256], BF16); inp("fft_h", [65, 128 * 2 * 64], BF16)
    inp("hy_zT", [33, L]); inp("hy_zT2", [33, L]); inp("hy_negd", [128, 16])
    scratch("modv", [2, 6 * D], F32)
    scratch("qT", [512, L], BF16)
    scratch("kT", [512, NKEY], BF16)
    scratch("vv", [NKEY, 512], BF16)
    scratch("uT", [1536, L], F32)
    scratch("hk", [1024, 2 * L], BF16)
    scratch("z2T", [512, L], F32)
    scratch("mixT", [512, L], BF16)
    scratch("x2", [L, D], F32)
    io.out = nc.dram_tensor("out", [L, D], F32, kind="ExternalOutput").ap()
    return io


def phase_mod(P, io, G):
    nc = P.nc
    cs = P.sb("cs", [128, 2, 8], F32); r_cs = P.res()
    P.dma("sp", cs[:, 0, :], io.c.rearrange("(j p) -> p j", p=128), writes=[r_cs], allow_slow_non_contiguous=True)
    P.dma("sp", cs[:, 1, :], io.c_ctx.rearrange("(j p) -> p j", p=128), writes=[r_cs], allow_slow_non_contiguous=True)
    ss = P.sb("ss", [128, 2, 8], F32); r_ss = P.res()
    P.op("act", lambda e: e.activation(out=ss[:], in_=cs[:], func=AF.Silu), reads=[r_cs], writes=[r_ss])
    bm = P.sb("bm", [128, 48], F32); r_bm = P.res()
    P.dma("sp", bm[:], io.b_mod.rearrange("(m p) -> p m", p=128), writes=[r_bm], allow_slow_non_contiguous=True)
    nm = P.sb("nm", [128, 2, 8], F32); r_nm = P.res()
    P.dma("sp", nm[:, 0, :], io.norm_mix.rearrange("(j p) -> p j", p=128), writes=[r_nm], allow_slow_non_contiguous=True)
    P.dma("sp", nm[:, 1, :], io.norm_ffn.rearrange("(j p) -> p j", p=128), writes=[r_nm], allow_slow_non_contiguous=True)

    wm = Rot(P, "wm", [128, 6 * D], F32, 2)
    psm = Rot(P, "psm", [128, 2, 48], F32, 2, psum=True)
    acc = G.mods; r_acc = G.r_mods
    for j in range(8):
        wt, r_w = wm.next()
        for s in range(4):
            q = "sp" if s % 2 == 0 else "act"
            P.dma(q, wt[:, s * 1536:(s + 1) * 1536], io.w_mod[j * 128:(j + 1) * 128, s * 1536:(s + 1) * 1536], writes=[r_w])
        pt, r_p = psm.next()
        for m in range(48):
            P.op("pe", lambda e, m=m, wt=wt, pt=pt, j=j: e.matmul(pt[:, :, m], wt[:, m * 128:(m + 1) * 128], ss[:, :, j],
                                                            start=True, stop=True),
                 reads=[r_w, r_ss], writes=[r_p])
        if j == 0:
            P.op("dve", lambda e, pt=pt: e.tensor_tensor(out=acc[:], in0=pt[:], in1=bm[:].unsqueeze(1).broadcast_to([128, 2, 48]),
                                                     op=ALU.add), reads=[r_p, r_bm], writes=[r_acc])
        else:
            P.op("dve", lambda e, pt=pt: e.tensor_tensor(out=acc[:], in0=pt[:], in1=acc[:], op=ALU.add),
                 reads=[r_p, r_acc], writes=[r_acc])
    ptr = P.ps("modtr", [96, 128], F32); r_ptr = P.res()
    P.op("pe", lambda e: e.transpose(ptr[:], acc[:].rearrange("p w m -> p (w m)"), G.ident_f[:]),
         reads=[r_acc, G.r_const], writes=[r_ptr])
    mrow = P.sb("modrow", [96, 128], F32); r_mrow = P.res()
    P.op("dve", lambda e: e.tensor_copy(mrow[:], ptr[:]), reads=[r_ptr], writes=[r_mrow])
    for w in range(2):
        P.dma("sp", io.modv[w].rearrange("(m p) -> m p", p=128), mrow[w * 48:(w + 1) * 48, :], reads=[r_mrow], writes=[G.r_modv])
    for (gt, sc0, ni) in ((G.gainA, 8, 0), (G.gainF, 32, 1)):
        P.op("dve", lambda e, gt=gt, sc0=sc0, ni=ni: e.scalar_tensor_tensor(
            out=gt[:], in0=acc[:, :, sc0:sc0 + 8], scalar=1.0, in1=nm[:, ni, :].unsqueeze(1).broadcast_to([128, 2, 8]),
            op0=ALU.add, op1=ALU.mult), reads=[r_acc, r_nm], writes=[G.r_gain])


def alloc_globals(P, G):
    G.ident_f = P.sb("ident_f", [128, 128], F32, glob=True)
    G.ident_b = P.sb("ident_b", [128, 128], BF16, glob=True)
    G.eps = P.sb("epsc", [128, 1], F32, glob=True)
    G.mods = P.sb("modacc", [128, 2, 48], F32, glob=True)
    G.gainA = P.sb("gainA", [128, 2, 8], F32, glob=True)
    G.gainF = P.sb("gainF", [128, 2, 8], F32, glob=True)
    G.r_const = P.res("const"); G.r_mods = P.res("mods"); G.r_gain = P.res("gain")


def load_consts(P, io, G):
    P.dma("sp", G.ident_f[:], io.ident_f[:, :], writes=[G.r_const])
    P.dma("sp", G.ident_b[:], io.ident_b[:, :], writes=[G.r_const])
    P.op("dve", lambda e: e.memset(G.eps[:], EPS), writes=[G.r_const])


def norm_tile(P, G, xt, r_x, npart, xn, r_xn, T):
    junk, r_junk = T["junk"].next()
    st, r_st = T["stat"].next()
    P.op("act", lambda e: e.activation(out=junk[:npart, :], in_=xt[:npart, :], func=AF.Square, accum_out=st[:npart, 0:1]),
         reads=[r_x], writes=[r_junk, r_st])
    P.op("act", lambda e: e.activation(out=st[:npart, 1:2], in_=st[:npart, 0:1], func=AF.Sqrt, scale=1.0 / D, bias=G.eps[:npart, :]),
         reads=[r_st, G.r_const], writes=[r_st])
    P.op("dve", lambda e: e.reciprocal(out=st[:npart, 2:3], in_=st[:npart, 1:2]), reads=[r_st], writes=[r_st])
    P.op("dve", lambda e: e.tensor_scalar(out=xn[:npart, :], in0=xt[:npart, :], scalar1=st[:npart, 2:3], scalar2=None,
                                          op0=ALU.mult), reads=[r_x, r_st], writes=[r_xn])


def phase_proj(P, io, G, nchunks=NCH):
    nc = P.nc
    wbf = P.sb("wbf", [128, 8, 3072], BF16); r_wbf = P.res()
    wpm = P.sb("wpm", [128, 8, 1024], BF16); r_wpm = P.res()
    wst = Rot(P, "wst", [128, 3072], F32, 2)
    for k in range(8):
        st, r_st = wst.next()
        for s in range(2):
            P.dma("sp" if s == 0 else "act", st[:, s * 1536:(s + 1) * 1536], io.w_in[k * 128:(k + 1) * 128, s * 1536:(s + 1) * 1536],
                  writes=[r_st])
        P.op("pool", lambda e, st=st, k=k: e.tensor_copy(wbf[:, k, :], st[:]), reads=[r_st], writes=[r_wbf])
        sv = st[:, 0:1024].rearrange("p (g t s) -> p g t s", t=2, s=16)
        dv = wpm[:, k, :].rearrange("p (g t s) -> p g t s", t=2, s=16)
        P.op("dve", lambda e, sv=sv, dv=dv: e.tensor_copy(dv[:, :, 0, :], sv[:, :, 1, :]), reads=[r_st], writes=[r_wpm])
        P.op("dve", lambda e, sv=sv, dv=dv: e.tensor_copy(dv[:, :, 1, :], sv[:, :, 0, :]), reads=[r_st], writes=[r_wpm])
    cw = P.sb("hycw", [128, 12, 4], F32); r_cw = P.res()
    for i in range(3):
        P.dma("sp", cw[:, :, i], io.hy_conv_w[i].rearrange("(j p) -> p j", p=128), writes=[r_cw], allow_slow_non_contiguous=True)
    P.dma("sp", cw[:, :, 3], io.hy_conv_b.rearrange("(j p) -> p j", p=128), writes=[r_cw], allow_slow_non_contiguous=True)

    T = {"junk": Rot(P, "junk", [128, D], F32, 1), "stat": Rot(P, "stat", [128, 4], F32, 4)}
    xts = Rot(P, "xt", [128, D], F32, 3)
    xns = Rot(P, "xn", [128, D], BF16, 2)
    hxs = Rot(P, "hxT", [128, 8, 514], BF16, 2)
    pTs = Rot(P, "pT", [128, 8, 128], BF16, 2, psum=True)
    psA = Rot(P, "psA", [128, 512], F32, 2, psum=True)
    psB = Rot(P, "psB", [128, 512], F32, 2, psum=True)
    psH = Rot(P, "psH", [128, 2], F32, 1, psum=True)
    ropec = Rot(P, "ropec", [128, 512], F32, 2)
    ropes = Rot(P, "ropes", [128, 512], F32, 2)
    t1s = Rot(P, "t1s", [128, 512], F32, 2)
    t2s = Rot(P, "t2s", [128, 512], F32, 2)
    qks = Rot(P, "qks", [128, 512], BF16, 3)
    vst = Rot(P, "vst", [128, 512], BF16, 3)
    exts = Rot(P, "ext", [128, 514], F32, 2)
    ust = Rot(P, "ust", [128, 512], F32, 3)
    halo = Rot(P, "halo", [2, D], F32, 2)
    halon = Rot(P, "halon", [2, D], BF16, 2)
    flip = [0]

    def transposed_modulated(xn, r_xn, npart, hx, r_hx, col0, w):
        pT, r_pT = pTs.next()
        for k in range(8):
            P.op("pe", lambda e, k=k, pT=pT: e.transpose(pT[:, k, 0:npart], xn[:npart, k * 128:(k + 1) * 128],
                                                         G.ident_b[:npart, :npart]),
                 reads=[r_xn, G.r_const], writes=[r_pT])
        for k in range(8):
            flip[0] ^= 1
            if flip[0]:
                P.op("act", lambda e, k=k, pT=pT: e.activation(out=hx[:, k, col0:col0 + npart], in_=pT[:, k, 0:npart],
                                                              func=AF.Identity, scale=G.gainA[:, w, k:k + 1],
                                                              bias=G.mods[:, w, k:k + 1]),
                     reads=[r_pT, G.r_gain, G.r_mods], writes=[r_hx])
            else:
                P.op("dve", lambda e, k=k, pT=pT: e.tensor_scalar(out=hx[:, k, col0:col0 + npart], in0=pT[:, k, 0:npart],
                                                                 scalar1=G.gainA[:, w, k:k + 1], scalar2=G.mods[:, w, k:k + 1],
                                                                 op0=ALU.mult, op1=ALU.add),
                     reads=[r_pT, G.r_gain, G.r_mods], writes=[r_hx])

    hc, r_hc = hxs.next()
    for tt in range(2):
        xt, r_x = xts.next()
        P.dma("sp", xt[:], io.ctx[tt * 128:(tt + 1) * 128, :], writes=[r_x])
        xn, r_xn = xns.next()
        norm_tile(P, G, xt, r_x, 128, xn, r_xn, T)
        transposed_modulated(xn, r_xn, 128, hc, r_hc, 1 + tt * 128, 1)
    for cc in range(4, 8):
        pa, r_pa = psA.next()
        for k in range(8):
            P.op("pe", lambda e, k=k, cc=cc, pa=pa: e.matmul(pa[:, 0:256], wbf[:, k, cc * 128:(cc + 1) * 128], hc[:, k, 1:257],
                                                            start=(k == 0), stop=(k == 7)), reads=[r_wbf, r_hc], writes=[r_pa])
        qk, r_qk = qks.next()
        P.op("act", lambda e, pa=pa, qk=qk: e.copy(out=qk[:, 0:256], in_=pa[:, 0:256]), reads=[r_pa], writes=[r_qk])
        P.dma("sp", io.kT[(cc - 4) * 128:(cc - 3) * 128, 0:256], qk[:, 0:256], reads=[r_qk], writes=[G.r_kT])
    for tt in range(2):
        pa, r_pa = psA.next()
        for k in range(8):
            P.op("pe", lambda e, k=k, tt=tt, pa=pa: e.matmul(pa[:], hc[:, k, 1 + tt * 128:1 + (tt + 1) * 128], wbf[:, k, 1024:1536],
                                                            start=(k == 0), stop=(k == 7)), reads=[r_wbf, r_hc], writes=[r_pa])
        vs, r_vs = vst.next()
        P.op("act", lambda e, pa=pa, vs=vs: e.copy(out=vs[:], in_=pa[:]), reads=[r_pa], writes=[r_vs])
        P.dma("sp", io.vv[tt * 128:(tt + 1) * 128, :], vs[:], reads=[r_vs], writes=[G.r_vv])

    def front(i):
        t0 = i * 512
        hx, r_hx = hxs.next()
        hl, r_hl = halo.next()
        if i == 0 or i == NCH - 1:
            P.op("pool", lambda e, hl=hl: e.memset(hl[0:2, :], 0.0), writes=[r_hl])
        if i != 0:
            P.dma("sp", hl[0:1, :], io.x[t0 - 1:t0, :], writes=[r_hl])
        if i != NCH - 1:
            P.dma("sp", hl[1:2, :], io.x[t0 + 512:t0 + 513, :], writes=[r_hl])
        hn, r_hn = halon.next()
        norm_tile(P, G, hl, r_hl, 2, hn, r_hn, T)
        pT, r_pT = pTs.next()
        for k in range(8):
            P.op("pe", lambda e, k=k, pT=pT, hn=hn: e.transpose(pT[:, k, 0:2], hn[0:2, k * 128:(k + 1) * 128], G.ident_b[0:2, 0:2]),
                 reads=[r_hn, G.r_const], writes=[r_pT])
        for k in range(8):
            P.op("dve", lambda e, k=k, pT=pT, hx=hx: e.tensor_scalar(
                out=hx[:, k, 0:514:513], in0=pT[:, k, 0:2], scalar1=G.gainA[:, 0, k:k + 1], scalar2=G.mods[:, 0, k:k + 1],
                op0=ALU.mult, op1=ALU.add), reads=[r_pT, G.r_gain, G.r_mods], writes=[r_hx])
        if i == 0:
            P.op("dve", lambda e, hx=hx: e.memset(hx[:, :, 0:1], 0.0), writes=[r_hx])
        if i == NCH - 1:
            P.op("dve", lambda e, hx=hx: e.memset(hx[:, :, 513:514], 0.0), writes=[r_hx])
        for tt in range(4):
            xt, r_x = xts.next()
            P.dma("sp", xt[:], io.x[t0 + tt * 128:t0 + (tt + 1) * 128, :], writes=[r_x])
            xn, r_xn = xns.next()
            norm_tile(P, G, xt, r_x, 128, xn, r_xn, T)
            transposed_modulated(xn, r_xn, 128, hx, r_hx, 1 + tt * 128, 0)
        return hx, r_hx

    def back(i, hx, r_hx):
        t0 = i * 512
        rc, r_rc = ropec.next(); rs, r_rs = ropes.next()
        P.dma("sp", rc[:], io.rope_c[:, t0:t0 + 512], writes=[r_rc])
        P.dma("sp", rs[:], io.rope_s[:, t0:t0 + 512], writes=[r_rs])
        for cc in range(8):
            pa, r_pa = psA.next(); pb, r_pb = psB.next()
            for k in range(8):
                P.op("pe", lambda e, k=k, cc=cc, pa=pa, hx=hx: e.matmul(pa[:], wbf[:, k, cc * 128:(cc + 1) * 128], hx[:, k, 1:513],
                                                                       start=(k == 0), stop=(k == 7)),
                     reads=[r_wbf, r_hx], writes=[r_pa])
            for k in range(8):
                P.op("pe", lambda e, k=k, cc=cc, pb=pb, hx=hx: e.matmul(pb[:], wpm[:, k, cc * 128:(cc + 1) * 128], hx[:, k, 1:513],
                                                                       start=(k == 0), stop=(k == 7)),
                     reads=[r_wpm, r_hx], writes=[r_pb])
            t1, r_t1 = t1s.next(); t2, r_t2 = t2s.next()
            P.op("dve", lambda e, pa=pa, t1=t1, rc=rc: e.tensor_tensor(out=t1[:], in0=pa[:], in1=rc[:], op=ALU.mult),
                 reads=[r_pa, r_rc], writes=[r_t1])
            P.op("dve", lambda e, pb=pb, t2=t2, rs=rs: e.tensor_tensor(out=t2[:], in0=pb[:], in1=rs[:], op=ALU.mult),
                 reads=[r_pb, r_rs], writes=[r_t2])
            qk, r_qk = qks.next()
            P.op("dve", lambda e, t1=t1, t2=t2, qk=qk: e.tensor_tensor(out=qk[:], in0=t1[:], in1=t2[:], op=ALU.add),
                 reads=[r_t1, r_t2], writes=[r_qk])
            if cc < 4:
                P.dma("sp", io.qT[cc * 128:(cc + 1) * 128, t0:t0 + 512], qk[:], reads=[r_qk], writes=[G.r_qT])
            else:
                P.dma("sp", io.kT[(cc - 4) * 128:(cc - 3) * 128, CTX + t0:CTX + t0 + 512], qk[:], reads=[r_qk], writes=[G.r_kT])
        for tt in range(4):
            pa, r_pa = psA.next()
            for k in range(8):
                P.op("pe", lambda e, k=k, tt=tt, pa=pa, hx=hx: e.matmul(pa[:], hx[:, k, 1 + tt * 128:1 + (tt + 1) * 128],
                                                                       wbf[:, k, 1024:1536], start=(k == 0), stop=(k == 7)),
                     reads=[r_wbf, r_hx], writes=[r_pa])
            vs, r_vs = vst.next()
            P.op("act", lambda e, pa=pa, vs=vs: e.copy(out=vs[:], in_=pa[:]), reads=[r_pa], writes=[r_vs])
            P.dma("sp", io.vv[CTX + t0 + tt * 128:CTX + t0 + (tt + 1) * 128, :], vs[:], reads=[r_vs], writes=[G.r_vv])
        for cc in range(12):
            c0 = 1536 + cc * 128
            pb, r_pb = psB.next(); ph, r_ph = psH.next()
            for k in range(8):
                P.op("pe", lambda e, k=k, c0=c0, pb=pb, hx=hx: e.matmul(pb[:], wbf[:, k, c0:c0 + 128], hx[:, k, 1:513],
                                                                       start=(k == 0), stop=(k == 7)),
                     reads=[r_wbf, r_hx], writes=[r_pb])
            for k in range(8):
                P.op("pe", lambda e, k=k, c0=c0, ph=ph, hx=hx: e.matmul(ph[:], wbf[:, k, c0:c0 + 128], hx[:, k, 0:514:513],
                                                                       start=(k == 0), stop=(k == 7)),
                     reads=[r_wbf, r_hx], writes=[r_ph])
            ex, r_ex = exts.next()
            P.op("act", lambda e, pb=pb, ex=ex: e.copy(out=ex[:, 1:513], in_=pb[:]), reads=[r_pb], writes=[r_ex])
            P.op("act", lambda e, ph=ph, ex=ex: e.copy(out=ex[:, 0:514:513], in_=ph[:]), reads=[r_ph], writes=[r_ex])
            us, r_us = ust.next()
            P.op("dve", lambda e, ex=ex, us=us, cc=cc: e.tensor_scalar(out=us[:], in0=ex[:, 1:513], scalar1=cw[:, cc, 1:2],
                                                                      scalar2=cw[:, cc, 3:4], op0=ALU.mult, op1=ALU.add),
                 reads=[r_ex, r_cw], writes=[r_us])
            P.op("dve", lambda e, ex=ex, us=us, cc=cc: e.scalar_tensor_tensor(out=us[:], in0=ex[:, 0:512], scalar=cw[:, cc, 0:1],
                                                                             in1=us[:], op0=ALU.mult, op1=ALU.add),
                 reads=[r_ex, r_cw, r_us], writes=[r_us])
            P.op("dve", lambda e, ex=ex, us=us, cc=cc: e.scalar_tensor_tensor(out=us[:], in0=ex[:, 2:514], scalar=cw[:, cc, 2:3],
                                                                             in1=us[:], op0=ALU.mult, op1=ALU.add),
                 reads=[r_ex, r_cw, r_us], writes=[r_us])
            P.dma("sp", io.uT[cc * 128:(cc + 1) * 128, t0:t0 + 512], us[:], reads=[r_us], writes=[G.r_uT])

    nxt = front(0)
    for i in range(nchunks):
        cur = nxt
        if i + 1 < nchunks:
            nxt = front(i + 1)
        back(i, *cur)


def build_program(dbg_out=(), dbg_in=(), phases=("mod", "proj", "filt", "hyena", "attn", "outproj", "ffn"), nchunks=NCH, nblocks=512 // NB, nheads=4, nqc=NCH):
    nc = bass.Bass("TRN2", target_bir_lowering=False)
    io = declare_io(nc, dbg_out, dbg_in)
    with ExitStack() as es:
        P = Prog(nc, es)
        G = IO()
        alloc_globals(P, G)
        plist = [("consts", lambda: load_consts(P, io, G)),
                 ("mod", lambda: phase_mod(P, io, G)),
                 ("proj", lambda: phase_proj(P, io, G, nchunks)),
                 ("filt", lambda: phase_filters(P, io, G)),
                 ("hyena", lambda: phase_hyena(P, io, G, nblocks)),
                 ("attn", lambda: phase_attn(P, io, G, nheads, nqc)),
                 ("outproj", lambda: phase_outproj(P, io, G, nchunks)),
                 ("ffn", lambda: phase_ffn(P, io, G, nchunks))]
        for nm in ("r_uT", "r_hk", "r_qT", "r_kT", "r_vv", "r_z2T", "r_modv", "r_mixT", "r_x2", "r_out"):
            setattr(G, nm, P.res(nm))
        for name, fn in plist:
            if name not in phases and name != "consts":
                continue
            with ExitStack() as pes:
                P.pes = pes
                fn()
                P.end_phase()
    return nc


def host_consts():
    C, S = rope_tables()
    return {
        "ident_f": np.eye(128, dtype=np.float32),
        "ident_b": np.eye(128, dtype=np.float32).astype(ml_dtypes.bfloat16),
        "rope_c": C, "rope_s": S, **fft_tables(), **hyena_consts(),
    }


def make_in_maps(inputs):
    consts = host_consts()
    shared = {}
    for k, v in inputs.items():
        if k in ("x", "c", "ctx"):
            continue
        a = np.asarray(v)
        if k not in ("c_ctx", "final_norm"):
            a = a[0]
        shared[k] = np.ascontiguousarray(a)
    shared.update(consts)
    maps = []
    for b in range(8):
        m = dict(shared)
        m["x"] = np.ascontiguousarray(np.asarray(inputs["x"])[b])
        m["c"] = np.ascontiguousarray(np.asarray(inputs["c"])[b])
        m["ctx"] = np.ascontiguousarray(np.asarray(inputs["ctx"])[b])
        maps.append(m)
    return maps


def kernel(**inputs):
    nc = build_program()
    maps = make_in_maps(inputs)
    res = run_bass_kernel_spmd(nc, maps, core_ids=list(range(8)))
    return np.stack([r["out"] for r in res.results], axis=0)


def fft_tables():
    N = NFFT
    a = np.arange(128)[:, None]; k1 = np.arange(65)[None, :]
    psi = 2 * np.pi * ((a * k1) % 128) / 128
    D1 = np.stack([np.cos(psi), -np.sin(psi), np.sin(psi), np.cos(psi)], 1).reshape(128, 260)
    b = np.arange(128)[:, None, None]; k1 = np.arange(65)[None, :, None]; k2 = np.arange(128)[None, None, :]
    phi = 2 * np.pi * ((b * (k1 + 128 * k2)) % N) / N
    G = np.stack([np.cos(phi), -np.sin(phi)], 2).reshape(128, 65 * 2 * 128)
    k2 = np.arange(128)[:, None]; bb = np.arange(128)[None, :]
    th = 2 * np.pi * ((k2 * bb) % 128) / 128
    E1 = np.stack([np.cos(th), np.sin(th)], 1).reshape(128, 256)
    E2 = np.stack([-np.sin(th), np.cos(th)], 1).reshape(128, 256)
    k1 = np.arange(65)[:, None, None]; b = np.arange(128)[None, :, None]; a = np.arange(64)[None, None, :]
    chi = 2 * np.pi * (((128 * a + b) * k1) % N) / N
    ck = np.where((k1 == 0) | (k1 == 64), 1.0, 2.0) / N
    H = np.stack([ck * np.cos(chi), -ck * np.sin(chi)], 2).reshape(65, 128 * 2 * 64)
    bf = ml_dtypes.bfloat16
    return {"fft_d1": D1.astype(np.float32).astype(bf), "fft_g": G.astype(np.float32).astype(bf),
            "fft_e1": E1.astype(np.float32).astype(bf), "fft_e2": E2.astype(np.float32).astype(bf),
            "fft_h": H.astype(np.float32).astype(bf)}


def hyena_consts():
    t = np.linspace(0.0, 1.0, L, dtype=np.float32)[:, None]
    w = (2.0 * np.float32(math.pi) * np.arange(L, dtype=np.float32)[:, None] / np.float32(L)).astype(np.float32)
    f = np.linspace(1e-4, 15, 16, dtype=np.float32)[None, :]
    z = np.concatenate([t, np.cos(f * w), -np.sin(f * w)], axis=-1).astype(np.float32)
    min_decay = math.log(1e-2) / 0.3
    max_decay = math.log(1e-2) / 1.5
    deltas = np.abs(np.linspace(min_decay, max_decay, 2048, dtype=np.float32))
    negd = (-deltas).reshape(16, 128).T.copy()
    zT = np.ascontiguousarray(z.T)
    zT2 = zT.copy()
    zT2[:, 1:] = zT[:, :0:-1]
    return {"hy_zT": zT, "hy_zT2": np.ascontiguousarray(zT2), "hy_negd": negd.astype(np.float32)}


def phase_filters(P, io, G):
    w1 = P.sb("fw1", [33, 64], F32); w2 = P.sb("fw2", [64, 64], F32); w3 = P.sb("fw3", [64, 64], F32)
    w4 = P.sb("fw4", [64, 2048], F32); r_w = P.res()
    P.dma("sp", w1[:], io.hy_w1[:, :], writes=[r_w]); P.dma("sp", w2[:], io.hy_w2[:, :], writes=[r_w])
    P.dma("sp", w3[:], io.hy_w3[:, :], writes=[r_w]); P.dma("act", w4[:], io.hy_w4[:, :], writes=[r_w])
    fr = P.sb("ffr", [64, 8], F32); r_fr = P.res()
    P.dma("sp", fr[:, 0:1], io.hy_freq.rearrange("(p o) -> p o", o=1), writes=[r_fr])
    for i, bb in enumerate((io.hy_b1, io.hy_b2, io.hy_b3)):
        P.dma("sp", fr[:, 1 + i:2 + i], bb.rearrange("(p o) -> p o", o=1), writes=[r_fr])
    P.op("dve", lambda e: e.tensor_scalar(out=fr[:, 4:7], in0=fr[:, 1:4], scalar1=fr[:, 0:1], scalar2=None, op0=ALU.mult),
         reads=[r_fr], writes=[r_fr])
    negd = P.sb("fnegd", [128, 16], F32); r_nd = P.res()
    P.dma("sp", negd[:], io.hy_negd[:, :], writes=[r_nd])
    zts = Rot(P, "fzT", [33, 512], F32, 2)
    tnb = Rot(P, "ftn", [128, 512], F32, 2)
    hs = [Rot(P, f"fh{i}", [64, 512], F32, 2) for i in range(3)]
    pre = Rot(P, "fpre", [64, 512], F32, 2); qi = Rot(P, "fqi", [64, 512], mybir.dt.int32, 2)
    qf = Rot(P, "fqf", [64, 512], F32, 2)
    psl = Rot(P, "fps", [64, 512], F32, 2, psum=True)
    ps4 = Rot(P, "fps4", [128, 512], F32, 3, psum=True)
    wins = Rot(P, "fwin", [128, 512], F32, 3)
    hst = Rot(P, "fhst", [128, 512], BF16, 4)
    ws = [w1, w2, w3]
    for i in range(NCH):
      for d in range(2):
        t0 = i * 512
        src = io.hy_zT if d == 0 else io.hy_zT2
        zt, r_zt = zts.next()
        P.dma("sp", zt[:], src[:, t0:t0 + 512], writes=[r_zt])
        tn, r_tn = tnb.next()
        P.dma("sp", tn[:], src[0:1, t0:t0 + 512].partition_broadcast(128), writes=[r_tn])
        cur, r_cur = zt, r_zt
        for l in range(3):
            ps, r_ps = psl.next()
            kk = 33 if l == 0 else 64
            P.op("pe", lambda e, ps=ps, l=l, cur=cur, kk=kk: e.matmul(ps[:], ws[l][:kk, :], cur[:kk, :], start=True, stop=True),
                 reads=[r_w, r_cur], writes=[r_ps])
            pr, r_pr = pre.next(); q1, r_q1 = qi.next(); q2, r_q2 = qf.next()
            P.op("dve", lambda e, ps=ps, pr=pr, l=l: e.tensor_scalar(out=pr[:], in0=ps[:], scalar1=fr[:, 0:1], scalar2=fr[:, 4 + l:5 + l],
                                                                  op0=ALU.mult, op1=ALU.add), reads=[r_ps, r_fr], writes=[r_pr])
            P.op("dve", lambda e, pr=pr, q1=q1: e.tensor_scalar(out=q1[:], in0=pr[:], scalar1=1.0 / TWO_PI, scalar2=None, op0=ALU.mult),
                 reads=[r_pr], writes=[r_q1])
            P.op("dve", lambda e, q1=q1, q2=q2: e.tensor_copy(q2[:], q1[:]), reads=[r_q1], writes=[r_q2])
            P.op("dve", lambda e, q2=q2, pr=pr: e.scalar_tensor_tensor(out=pr[:], in0=q2[:], scalar=-TWO_PI, in1=pr[:],
                                                                       op0=ALU.mult, op1=ALU.add), reads=[r_q2, r_pr], writes=[r_pr])
            P.op("dve", lambda e, pr=pr: e.tensor_scalar(out=pr[:], in0=pr[:], scalar1=PI_LO, scalar2=-PI_LO, op0=ALU.min, op1=ALU.max),
                 reads=[r_pr], writes=[r_pr])
            h, r_h = hs[l].next()
            P.op("act", lambda e, pr=pr, h=h: e.activation(out=h[:], in_=pr[:], func=AF.Sin), reads=[r_pr], writes=[r_h])
            cur, r_cur = h, r_h
        for j in range(16):
            if (j // 4) % 2 != d:
                continue
            n_, cc = j // 8, j % 4
            ps, r_ps = ps4.next()
            P.op("pe", lambda e, ps=ps, j=j, cur=cur: e.matmul(ps[:], w4[:, j * 128:(j + 1) * 128], cur[:], start=True, stop=True),
                 reads=[r_w, r_cur], writes=[r_ps])
            wn, r_wn = wins.next()
            P.op("act", lambda e, wn=wn, tn=tn, j=j: e.activation(out=wn[:], in_=tn[:], func=AF.Exp, scale=negd[:, j:j + 1]),
                 reads=[r_tn, r_nd], writes=[r_wn])
            st, r_st = hst.next()
            P.op("dve", lambda e, ps=ps, wn=wn, st=st: e.tensor_tensor(out=st[:], in0=ps[:], in1=wn[:], op=ALU.mult),
                 reads=[r_ps, r_wn], writes=[r_st])
            if i == 0 and d == 1:
                P.op("dve", lambda e, st=st: e.memset(st[:, 0:1], 0.0), writes=[r_st])
            r0 = n_ * 512 + cc * 128
            P.dma("sp", io.hk[r0:r0 + 128, d * L + t0:d * L + t0 + 512], st[:], reads=[r_st], writes=[G.r_hk])


def phase_hyena(P, io, G, nblocks=512 // NB):
    d1 = P.sb("d1", [128, 260], BF16); gt = P.sb("gt", [128, 65, 2, 128], BF16)
    e1 = P.sb("e1", [128, 256], BF16); e2 = P.sb("e2", [128, 256], BF16)
    ht = P.sb("ht", [65, 128, 2, 64], BF16); r_tab = P.res()
    P.dma("sp", d1[:], io.fft_d1[:, :], writes=[r_tab])
    for s in range(5):
        ks = slice(s * 13, (s + 1) * 13)
        P.dma("sp" if s % 2 == 0 else "act", gt[:, ks], io.fft_g.rearrange("p (k r m) -> p k r m", r=2, m=128)[:, ks], writes=[r_tab])
    P.dma("sp", e1[:], io.fft_e1[:, :], writes=[r_tab]); P.dma("sp", e2[:], io.fft_e2[:, :], writes=[r_tab])
    for s in range(4):
        bs = slice(s * 32, (s + 1) * 32)
        P.dma("act" if s % 2 == 0 else "sp", ht[:, bs], io.fft_h.rearrange("p (b r a) -> p b r a", r=2, a=64)[:, bs], writes=[r_tab])
    zins = Rot(P, "zin", [128, NB, 128], BF16, 2)
    tA = P.sb("gA", [64, NB, 128], F32); r_A = P.res()
    tB = P.sb("gB", [64, NB, 128], F32); r_B = P.res()
    Ys = P.sb("Ys", [128, NB, 4, 65], BF16); r_Ys = P.res()
    Kf = [P.sb(f"Kf{n}", [128, 65, 2, NB], F32) for n in range(2)]; r_Kf = [P.res(), P.res()]
    Ps = P.sb("Ps", [128, 2, NB, 65], BF16); r_Ps = P.res()
    U = P.sb("U", [65, NB, 2, 128], BF16); r_U = P.res()
    tmps = [Rot(P, f"ctmp{i}", [128, 8, NB], F32, 2) for i in range(4)]
    gtmp = Rot(P, "gtmp", [64, NB, 16], F32, 2)
    skb = P.sb("skb", [64, 2, NB], F32); r_skb = P.res()
    ps1 = Rot(P, "hps1", [128, 260], F32, 2, psum=True)
    ps2 = Rot(P, "hps2", [128, 8, 2, NB], F32, 2, psum=True)
    psa = Rot(P, "hpsa", [65, 256], F32, 2, psum=True)
    psb = Rot(P, "hpsb", [64, 16, NB], F32, 2, psum=True)
    flip = [0]

    def sig_ap(src, r0):
        return src[r0:r0 + NB, :].rearrange("c (a b) -> a c b", b=128)

    def fwd(zin, r_zin, consumer, kp=64):
        for c in range(NB):
            ps, r_ps = ps1.next()
            P.op("pe", lambda e, ps=ps, c=c: e.matmul(ps[:], zin[:kp, c, :], d1[:kp, :], start=True, stop=True),
                 reads=[r_zin, r_tab], writes=[r_ps])
            flip[0] ^= 1
            dst = Ys[:, c, :, :]
            src = ps[:].rearrange("p (q k) -> p q k", q=4)
            if flip[0]:
                P.op("act", lambda e, dst=dst, src=src: e.copy(out=dst, in_=src), reads=[r_ps], writes=[r_Ys])
            else:
                P.op("dve", lambda e, dst=dst, src=src: e.tensor_copy(dst, src), reads=[r_ps], writes=[r_Ys])
        for g in range(9):
            k1s = list(range(g * 8, min(65, g * 8 + 8)))
            ps, r_ps = ps2.next()
            for j, k1 in enumerate(k1s):
                P.op("pe", lambda e, ps=ps, j=j, k1=k1: e.matmul(ps[:, j], gt[:, k1, 0, :], Ys[:, :, 0:2, k1].rearrange("p c q -> p q c"), start=True, stop=False),
                     reads=[r_tab, r_Ys], writes=[r_ps])
                P.op("pe", lambda e, ps=ps, j=j, k1=k1: e.matmul(ps[:, j], gt[:, k1, 1, :], Ys[:, :, 2:4, k1].rearrange("p c q -> p q c"), start=False, stop=True),
                     reads=[r_tab, r_Ys], writes=[r_ps])
            consumer(g, k1s, ps, r_ps)

    def filt_first(n):
        def cons(g, k1s, ps, r_ps):
            nk = len(k1s)
            P.op("act", lambda e: e.copy(out=Kf[n][:, k1s[0]:k1s[0] + nk], in_=ps[:, 0:nk]), reads=[r_ps], writes=[r_Kf[n]])
        return cons

    def filt_second(n):
        def cons(g, k1s, ps, r_ps):
            nk = len(k1s); ks = slice(k1s[0], k1s[0] + nk)
            P.op("dve", lambda e: e.tensor_tensor(out=Kf[n][:, ks, 0, :], in0=Kf[n][:, ks, 0, :], in1=ps[:, 0:nk, 0, :], op=ALU.add),
                 reads=[r_ps, r_Kf[n]], writes=[r_Kf[n]])
            P.op("dve", lambda e: e.tensor_tensor(out=Kf[n][:, ks, 1, :], in0=Kf[n][:, ks, 1, :], in1=ps[:, 0:nk, 1, :], op=ALU.subtract),
                 reads=[r_ps, r_Kf[n]], writes=[r_Kf[n]])
        return cons

    def data_mul(n):
        def cons(g, k1s, ps, r_ps):
            nk = len(k1s); ks = slice(k1s[0], k1s[0] + nk)
            t = [tm.next() for tm in tmps]
            combos = [(0, 0), (1, 1), (0, 1), (1, 0)]
            for i, (xp, kp) in enumerate(combos):
                P.op("dve", lambda e, i=i, xp=xp, kp=kp: e.tensor_tensor(out=t[i][0][:, 0:nk, :], in0=ps[:, 0:nk, xp, :],
                                                                         in1=Kf[n][:, ks, kp, :], op=ALU.mult),
                     reads=[r_ps, r_Kf[n]], writes=[t[i][1]])
            P.op("pool", lambda e: e.tensor_tensor(out=Ps[:, 0, :, ks].rearrange("p c k -> p k c"), in0=t[0][0][:, 0:nk, :],
                                                   in1=t[1][0][:, 0:nk, :], op=ALU.subtract),
                 reads=[t[0][1], t[1][1]], writes=[r_Ps])
            P.op("pool", lambda e: e.tensor_tensor(out=Ps[:, 1, :, ks].rearrange("p c k -> p k c"), in0=t[2][0][:, 0:nk, :],
                                                   in1=t[3][0][:, 0:nk, :], op=ALU.add),
                 reads=[t[2][1], t[3][1]], writes=[r_Ps])
        return cons

    def inverse(gate):
        for c in range(NB):
            ps, r_ps = psa.next()
            P.op("pe", lambda e, ps=ps, c=c: e.matmul(ps[:], Ps[:, 0, c, :], e1[:], start=True, stop=False),
                 reads=[r_Ps, r_tab], writes=[r_ps])
            P.op("pe", lambda e, ps=ps, c=c: e.matmul(ps[:], Ps[:, 1, c, :], e2[:], start=False, stop=True),
                 reads=[r_Ps, r_tab], writes=[r_ps])
            flip[0] ^= 1
            dst = U[:, c, :, :]
            src = ps[:].rearrange("p (r b) -> p r b", r=2)
            if flip[0]:
                P.op("act", lambda e, dst=dst, src=src: e.copy(out=dst, in_=src), reads=[r_ps], writes=[r_U])
            else:
                P.op("dve", lambda e, dst=dst, src=src: e.tensor_copy(dst, src), reads=[r_ps], writes=[r_U])
        for g in range(8):
            ps, r_ps = psb.next()
            for j in range(16):
                b = g * 16 + j
                P.op("pe", lambda e, ps=ps, j=j, b=b: e.matmul(ps[:, j, :], ht[:, b, 0, :], U[:, :, 0, b], start=True, stop=False),
                     reads=[r_tab, r_U], writes=[r_ps])
                P.op("pe", lambda e, ps=ps, j=j, b=b: e.matmul(ps[:, j, :], ht[:, b, 1, :], U[:, :, 1, b], start=False, stop=True),
                     reads=[r_tab, r_U], writes=[r_ps])
            gate(g, ps, r_ps)

    for blk in range(nblocks):
        c0 = blk * NB
        P.dma("sp", skb[:, 0, :], io.hy_skip[0:1, c0:c0 + NB].partition_broadcast(64), writes=[r_skb])
        P.dma("sp", skb[:, 1, :], io.hy_skip[1:2, c0:c0 + NB].partition_broadcast(64), writes=[r_skb])
        for n in range(2):
            zin, r_zin = zins.next()
            P.dma("sp", zin[:], sig_ap(io.hk, n * 512 + c0), reads=[G.r_hk], writes=[r_zin])
            fwd(zin, r_zin, filt_first(n), kp=128)
        P.dma("sp", tA[:], sig_ap(io.uT, c0), reads=[G.r_uT], writes=[r_A])
        P.dma("sp", tB[:], sig_ap(io.uT, 512 + c0), reads=[G.r_uT], writes=[r_B])
        zin, r_zin = zins.next()
        P.op("pool", lambda e, zin=zin: e.tensor_copy(zin[:64], tA[:]), reads=[r_A], writes=[r_zin])
        fwd(zin, r_zin, data_mul(0))
        P.op("pool", lambda e: e.tensor_tensor(out=tA[:], in0=tA[:], in1=skb[:, 0, :].unsqueeze(2).broadcast_to([64, NB, 128]), op=ALU.mult),
             reads=[r_A, r_skb], writes=[r_A])
        zin2, r_zin2 = zins.next()

        def gate0(g, ps, r_ps, zin2=zin2, r_zin2=r_zin2):
            bs = slice(g * 16, (g + 1) * 16)
            tm, r_tm = gtmp.next()
            P.op("dve", lambda e: e.tensor_tensor(out=tm[:], in0=ps[:].rearrange("p b c -> p c b"), in1=tA[:, :, bs], op=ALU.add),
                 reads=[r_ps, r_A], writes=[r_tm])
            P.op("pool", lambda e: e.tensor_tensor(out=tB[:, :, bs], in0=tm[:], in1=tB[:, :, bs], op=ALU.mult),
                 reads=[r_tm, r_B], writes=[r_B])
            P.op("act", lambda e: e.copy(out=zin2[:64, :, bs], in_=tB[:, :, bs]), reads=[r_B], writes=[r_zin2])
        inverse(gate0)
        P.dma("sp", tA[:], sig_ap(io.uT, 1024 + c0), reads=[G.r_uT], writes=[r_A])
        fwd(zin2, r_zin2, data_mul(1))
        P.op("pool", lambda e: e.tensor_tensor(out=tB[:], in0=tB[:], in1=skb[:, 1, :].unsqueeze(2).broadcast_to([64, NB, 128]), op=ALU.mult),
             reads=[r_B, r_skb], writes=[r_B])

        def gate1(g, ps, r_ps):
            bs = slice(g * 16, (g + 1) * 16)
            tm, r_tm = gtmp.next()
            P.op("dve", lambda e: e.tensor_tensor(out=tm[:], in0=ps[:].rearrange("p b c -> p c b"), in1=tB[:, :, bs], op=ALU.add),
                 reads=[r_ps, r_B], writes=[r_tm])
            P.op("pool", lambda e: e.tensor_tensor(out=tA[:, :, bs], in0=tm[:], in1=tA[:, :, bs], op=ALU.mult),
                 reads=[r_tm, r_A], writes=[r_A])
        inverse(gate1)
        P.dma("sp", sig_ap(io.z2T, c0), tA[:], reads=[r_A], writes=[G.r_z2T])


def phase_attn(P, io, G, nheads=4, nqc=NCH):
    NKT = NKEY // 128
    ones_b = P.sb("ones_b", [128, 128], BF16); ones_f = P.sb("ones_f", [128, 128], F32); r_c = P.res()
    P.op("pool", lambda e: e.memset(ones_b[:], 1.0), writes=[r_c])
    P.op("pool", lambda e: e.memset(ones_f[:], 1.0 / 128.0), writes=[r_c])
    lq = P.sb("lamq", [128, 4, 64], F32); r_lq = P.res()
    for i, a in enumerate((io.lam_q1, io.lam_k1, io.lam_q2, io.lam_k2)):
        P.dma("sp", lq[:, i, :], a.rearrange("(o d) -> o d", o=1).partition_broadcast(128), writes=[r_lq])
    lt = P.sb("lamt", [128, 2, 64], F32); ls = P.sb("lams", [128, 4], F32); r_lt = P.res()
    P.op("dve", lambda e: e.tensor_tensor(out=lt[:], in0=lq[:, 0:4:2, :], in1=lq[:, 1:4:2, :], op=ALU.mult), reads=[r_lq], writes=[r_lt])
    P.op("dve", lambda e: e.reduce_sum(out=ls[:, 0:2], in_=lt[:], axis=AX.X), reads=[r_lt], writes=[r_lt])
    P.op("act", lambda e: e.activation(out=ls[:, 0:2], in_=ls[:, 0:2], func=AF.Exp), reads=[r_lt], writes=[r_lt])
    P.op("dve", lambda e: e.tensor_tensor(out=ls[:, 2:3], in0=ls[:, 1:2], in1=ls[:, 0:1], op=ALU.subtract), reads=[r_lt], writes=[r_lt])
    P.op("dve", lambda e: e.tensor_scalar(out=ls[:, 3:4], in0=ls[:, 2:3], scalar1=-LAM_INIT, scalar2=None, op0=ALU.add), reads=[r_lt], writes=[r_lt])
    neglam = ls[:, 3:4]
    sg = P.sb("subg", [128, 1], F32); r_sg = P.res()
    P.dma("sp", sg[:], io.subln.rearrange("(p o) -> p o", o=1), writes=[r_sg])
    P.op("dve", lambda e: e.tensor_scalar(out=sg[:], in0=sg[:], scalar1=1.0 - LAM_INIT, scalar2=None, op0=ALU.mult), reads=[r_sg], writes=[r_sg])

    ones_1 = P.sb("ones_1", [128, 128], F32)
    P.op("pool", lambda e: e.memset(ones_1[:], 1.0), writes=[r_c])
    KT = Rot(P, "aKT", [128, NKEY], BF16, 2)
    VV = Rot(P, "aVV", [128, NKT, 128], BF16, 2)
    QT = Rot(P, "aQT", [128, 512], BF16, 3)
    pTs = Rot(P, "apT", [128, 2, 512], BF16, 4)
    psS = Rot(P, "apsS", [128, 512], F32, 4, psum=True)
    psO = [P.ps(f"apsO{m}", [128, 512], F32) for m in range(2)]; r_O = [P.res(), P.res()]
    psX = Rot(P, "apsX", [128, 512], F32, 2, psum=True)
    accs = Rot(P, "aacc", [128, 2, 512], F32, 2)
    oraw = [Rot(P, f"aoraw{m}", [128, 512], F32, 2) for m in range(2)]
    rr = Rot(P, "arr", [128, 512], F32, 2)
    sqs = Rot(P, "asq", [128, 512], F32, 2)
    outs = Rot(P, "aout", [128, 512], BF16, 3)
    LA = 1
    steps = [(h, qc, kt) for h in range(nheads) for qc in range(nqc) for kt in range(NKT)]
    bufs = {}
    sbank = {}
    deferred = []
    cur_acc = {}

    def ensure_loaded(h, qc):
        if ("kv", h) not in bufs:
            kt_t, r_kt = KT.next(); vv_t, r_vv = VV.next()
            for s_ in range(4):
                cs = slice(s_ * (NKEY // 4), (s_ + 1) * (NKEY // 4))
                P.dma("sp" if s_ % 2 == 0 else "sp", kt_t[:, cs], io.kT[h * 128:(h + 1) * 128, cs], reads=[G.r_kT], writes=[r_kt])
            for s_ in range(6):
                ks = slice(s_ * 11, (s_ + 1) * 11)
                P.dma("sp" if s_ % 2 == 0 else "sp", vv_t[:, ks, :],
                      io.vv[s_ * 11 * 128:(s_ + 1) * 11 * 128, h * 128:(h + 1) * 128].rearrange("(k p) v -> p k v", p=128),
                      reads=[G.r_vv], writes=[r_vv])
            bufs[("kv", h)] = (kt_t, r_kt, vv_t, r_vv)
        if ("q", h, qc) not in bufs:
            q_t, r_q = QT.next()
            P.dma("sp", q_t[:], io.qT[h * 128:(h + 1) * 128, qc * 512:(qc + 1) * 512], reads=[G.r_qT], writes=[r_q])
            bufs[("q", h, qc)] = (q_t, r_q)

    def issue_qk(si):
        h, qc, kt = steps[si]
        ensure_loaded(h, qc)
        kt_t, r_kt, vv_t, r_vv = bufs[("kv", h)]
        q_t, r_q = bufs[("q", h, qc)]
        for m in range(2):
            ms = slice(m * 64, (m + 1) * 64)
            ps, r_ps = psS.next()
            sbank[(si, m)] = (ps, r_ps)
            P.op("pe", lambda e, ps=ps, ms=ms: e.matmul(ps[:], kt_t[ms, kt * 128:(kt + 1) * 128], q_t[ms, :], start=True, stop=True),
                 reads=[r_kt, r_q], writes=[r_ps])

    def epilogue1(h, qc, ac, orw):
        o_n = []
        for m in range(2):
            px, r_px = psX.next()
            P.op("pe", lambda e, px=px, m=m: e.matmul(px[:], ones_1[:], ac[0][:, m, :], start=True, stop=True),
                 reads=[r_c, ac[1]], writes=[r_px])
            r_t, r_r = rr.next()
            P.op("dve", lambda e, px=px, r_t=r_t: e.reciprocal(out=r_t[:], in_=px[:]), reads=[r_px], writes=[r_r])
            ot, r_ot = orw[m]
            P.op("dve", lambda e, ot=ot, r_t=r_t: e.tensor_tensor(out=ot[:], in0=ot[:], in1=r_t[:], op=ALU.mult), reads=[r_ot, r_r], writes=[r_ot])
            o_n.append((ot, r_ot))
        (o0, r_o0), (o1, r_o1) = o_n
        P.op("dve", lambda e: e.scalar_tensor_tensor(out=o0[:], in0=o1[:], scalar=neglam, in1=o0[:], op0=ALU.mult, op1=ALU.add),
             reads=[r_o1, r_o0, r_lt], writes=[r_o0])
        sq, r_sq = sqs.next()
        P.op("pool", lambda e: e.tensor_tensor(out=sq[:], in0=o0[:], in1=o0[:], op=ALU.mult), reads=[r_o0], writes=[r_sq])
        return o0, r_o0, sq, r_sq

    def epilogue2(h, qc, o0, r_o0, sq, r_sq):
        px, r_px = psX.next()
        P.op("pe", lambda e: e.matmul(px[:], ones_f[:], sq[:], start=True, stop=True), reads=[r_c, r_sq], writes=[r_px])
        rs, r_rs = rr.next()
        P.op("act", lambda e: e.activation(out=rs[:], in_=px[:], func=AF.Sqrt, bias=G.eps[:], scale=1.0),
             reads=[r_px, G.r_const], writes=[r_rs])
        P.op("dve", lambda e: e.reciprocal(out=rs[:], in_=rs[:]), reads=[r_rs], writes=[r_rs])
        P.op("dve", lambda e: e.tensor_tensor(out=o0[:], in0=o0[:], in1=rs[:], op=ALU.mult), reads=[r_rs, r_o0], writes=[r_o0])
        ob, r_ob = outs.next()
        P.op("dve", lambda e: e.tensor_scalar(out=ob[:], in0=o0[:], scalar1=sg[:, 0:1], scalar2=None, op0=ALU.mult),
             reads=[r_o0, r_sg], writes=[r_ob])
        P.dma("sp", io.mixT[h * 128:(h + 1) * 128, qc * 512:(qc + 1) * 512], ob[:], reads=[r_ob], writes=[G.r_mixT])

    def issue_rest(si):
        h, qc, kt = steps[si]
        kt_t, r_kt, vv_t, r_vv = bufs[("kv", h)]
        if kt == 0:
            cur_acc[0] = accs.next()
        pT, r_pT = pTs.next()
        for m in range(2):
            ps, r_ps = sbank.pop((si, m))
            P.op("act", lambda e, ps=ps, m=m: e.activation(out=pT[:, m, :], in_=ps[:], func=AF.Exp, scale=0.125), reads=[r_ps], writes=[r_pT])
        for m in range(2):
            P.op("pe", lambda e, m=m: e.matmul(psO[m][:], vv_t[:, kt, :], pT[:, m, :], start=(kt == 0), stop=(kt == NKT - 1)),
                 reads=[r_vv, r_pT], writes=[r_O[m]])
        ac, r_ac = cur_acc[0]
        if kt == 0:
            P.op("dve", lambda e: e.tensor_copy(ac[:], pT[:]), reads=[r_pT], writes=[r_ac])
        else:
            P.op("dve", lambda e: e.tensor_tensor(out=ac[:], in0=ac[:], in1=pT[:], op=ALU.add), reads=[r_pT, r_ac], writes=[r_ac])
        if kt == NKT - 1:
            orw = []
            for m in range(2):
                ot, r_ot = oraw[m].next()
                P.op("act", lambda e, ot=ot, m=m: e.copy(out=ot[:], in_=psO[m][:]), reads=[r_O[m]], writes=[r_ot])
                orw.append((ot, r_ot))
            ac2 = cur_acc[0]

            def d1(h=h, qc=qc, ac2=ac2, orw=orw, si=si):
                args = epilogue1(h, qc, ac2, orw)
                deferred.append((si + 8, lambda: epilogue2(h, qc, *args)))
            deferred.append((si + 3, d1))

    n = len(steps)
    for si in range(min(LA, n)):
        issue_qk(si)
    for si in range(n):
        if si + LA < n:
            issue_qk(si + LA)
        issue_rest(si)
        while deferred and deferred[0][0] <= si:
            deferred.pop(0)[1]()
    while deferred:
        deferred.pop(0)[1]()


def load_weight_bf16(P, io_w, nk, ncols, dst, r_dst, scale_b=None, r_scale=None, name="wl", chunk=2048):
    st = Rot(P, name + "st", [128, chunk], F32, 2)
    n = 0
    for k in range(nk):
        for c0 in range(0, ncols, chunk):
            cw = min(chunk, ncols - c0)
            s, r_s = st.next()
            P.dma("sp" if n % 2 == 0 else "act", s[:, :cw], io_w[k * 128:(k + 1) * 128, c0:c0 + cw], writes=[r_s])
            eng = ("dve", "pool")[n % 2]
            if scale_b is None:
                P.op(eng, lambda e, s=s, k=k, c0=c0, cw=cw: e.tensor_copy(dst[:, k, c0:c0 + cw], s[:, :cw]), reads=[r_s], writes=[r_dst])
            else:
                P.op(eng, lambda e, s=s, k=k, c0=c0, cw=cw: e.tensor_tensor(out=dst[:, k, c0:c0 + cw], in0=s[:, :cw],
                                                                           in1=scale_b[:, c0:c0 + cw], op=ALU.mult),
                     reads=[r_s, r_scale], writes=[r_dst])
            n += 1


def phase_outproj(P, io, G, nchunks=NCH):
    gab = P.sb("gab", [128, D], F32); r_gab = P.res()
    P.dma("sp", gab[:], io.modv[0:1, 2 * D:3 * D].partition_broadcast(128), reads=[G.r_modv], writes=[r_gab])
    wo = P.sb("wo", [128, 8, D], BF16); r_wo = P.res()
    load_weight_bf16(P, io.w_out, 8, D, wo, r_wo, gab, r_gab, name="wo", chunk=1024)
    hn = P.sb("hyn", [128, 4], F32); r_hn = P.res()
    P.dma("sp", hn[:], io.hy_norm.rearrange("(j p) -> p j", p=128), writes=[r_hn], allow_slow_non_contiguous=True)
    ones_f = P.sb("ones_f2", [128, 128], F32); r_c = P.res()
    P.op("pool", lambda e: e.memset(ones_f[:], 1.0 / 512.0), writes=[r_c])
    z2s = Rot(P, "oz2", [128, 4, 512], F32, 2)
    sqs = Rot(P, "osq", [128, 4, 512], F32, 1)
    mix = Rot(P, "omix", [128, 8, 512], BF16, 2)
    rst = Rot(P, "orst", [128, 512], F32, 2)
    psM = Rot(P, "opsM", [128, 512], F32, 2, psum=True)
    psX = Rot(P, "opsX", [128, 512], F32, 4, psum=True)
    xts = Rot(P, "oxt", [128, D], F32, 8)
    def front(i):
        t0 = i * 512
        mx, r_mx = mix.next()
        P.dma("sp", mx[:, 0:4, :], io.mixT[:, t0:t0 + 512].rearrange("(k p) t -> p k t", p=128), reads=[G.r_mixT], writes=[r_mx])
        z2, r_z2 = z2s.next()
        P.dma("sp", z2[:], io.z2T[:, t0:t0 + 512].rearrange("(k p) t -> p k t", p=128), reads=[G.r_z2T], writes=[r_z2])
        sq, r_sq = sqs.next()
        P.op("dve", lambda e, sq=sq, z2=z2: e.tensor_tensor(out=sq[:], in0=z2[:], in1=z2[:], op=ALU.mult), reads=[r_z2], writes=[r_sq])
        pm, r_pm = psM.next()
        for k in range(4):
            P.op("pe", lambda e, pm=pm, sq=sq, k=k: e.matmul(pm[:], ones_f[:], sq[:, k, :], start=(k == 0), stop=(k == 3)),
                 reads=[r_c, r_sq], writes=[r_pm])
        rs, r_rs = rst.next()
        P.op("act", lambda e, pm=pm, rs=rs: e.activation(out=rs[:], in_=pm[:], func=AF.Sqrt, bias=G.eps[:], scale=1.0),
             reads=[r_pm, G.r_const], writes=[r_rs])
        P.op("dve", lambda e, rs=rs: e.reciprocal(out=rs[:], in_=rs[:]), reads=[r_rs], writes=[r_rs])
        for k in range(4):
            P.op("dve", lambda e, z2=z2, rs=rs, k=k: e.tensor_tensor(out=z2[:, k, :], in0=z2[:, k, :], in1=rs[:], op=ALU.mult),
                 reads=[r_z2, r_rs], writes=[r_z2])
            P.op("dve", lambda e, z2=z2, mx=mx, k=k: e.tensor_scalar(out=mx[:, 4 + k, :], in0=z2[:, k, :], scalar1=hn[:, k:k + 1], scalar2=None,
                                                                     op0=ALU.mult), reads=[r_z2, r_hn], writes=[r_mx])
        return mx, r_mx

    def back(i, mx, r_mx):
        t0 = i * 512
        for tt in range(4):
            xt, r_x = xts.next()
            P.dma("sp", xt[:], io.x[t0 + tt * 128:t0 + (tt + 1) * 128, :], writes=[r_x])
            for dh in range(2):
                px, r_px = psX.next()
                for k in range(8):
                    P.op("pe", lambda e, px=px, mx=mx, k=k, tt=tt, dh=dh: e.matmul(px[:], mx[:, k, tt * 128:(tt + 1) * 128],
                                                                                 wo[:, k, dh * 512:(dh + 1) * 512],
                                                                                 start=(k == 0), stop=(k == 7)),
                         reads=[r_mx, r_wo], writes=[r_px])
                P.op("dve", lambda e, px=px, xt=xt, dh=dh: e.tensor_tensor(out=xt[:, dh * 512:(dh + 1) * 512], in0=px[:],
                                                                        in1=xt[:, dh * 512:(dh + 1) * 512], op=ALU.add),
                     reads=[r_px, r_x], writes=[r_x])
            P.dma("sp", io.x2[t0 + tt * 128:t0 + (tt + 1) * 128, :], xt[:], reads=[r_x], writes=[G.r_x2])

    nxt = front(0)
    for i in range(nchunks):
        cur = nxt
        if i + 1 < nchunks:
            nxt = front(i + 1)
        back(i, *cur)


def phase_ffn(P, io, G, nchunks=NCH):
    NJ = DFF // 128
    fnb = P.sb("fnb", [128, D], F32); r_fnb = P.res()
    P.dma("sp", fnb[:], io.final_norm.rearrange("(o d) -> o d", o=1).partition_broadcast(128), writes=[r_fnb])
    wup = P.sb("wup", [128, 8, 2 * DFF], BF16); r_wup = P.res()
    wdn = P.sb("wdn", [128, NJ, D], BF16); r_wdn = P.res()
    cw = P.sb("fcw", [128, NJ, 4], F32); r_cw = P.res()
    for i in range(3):
        P.dma("sp", cw[:, :, i], io.ffn_conv_w[i].rearrange("(j p) -> p j", p=128), writes=[r_cw], allow_slow_non_contiguous=True)
    P.dma("sp", cw[:, :, 3], io.ffn_conv_b.rearrange("(j p) -> p j", p=128), writes=[r_cw], allow_slow_non_contiguous=True)
    with P.temp_scope():
        gfb = P.sb("gfb", [128, D], F32); r_gfb = P.res()
        P.dma("sp", gfb[:], io.modv[0:1, 5 * D:6 * D].partition_broadcast(128), reads=[G.r_modv], writes=[r_gfb])
        load_weight_bf16(P, io.ffn_w_up, 8, 2 * DFF, wup, r_wup, name="wu", chunk=1408)
        load_weight_bf16(P, io.ffn_w_down, NJ, D, wdn, r_wdn, gfb, r_gfb, name="wd", chunk=1024)

    T = {"junk": Rot(P, "fjunk", [128, D], BF16, 1), "stat": Rot(P, "fstat", [128, 4], F32, 4)}
    xts = Rot(P, "fxt", [128, D], F32, 1)
    xns = Rot(P, "fxn", [128, D], BF16, 2)
    hfs = Rot(P, "fhf", [128, 8, 514], BF16, 2)
    halo = Rot(P, "fhalo", [2, D], F32, 1)
    halon = Rot(P, "fhalon", [2, D], BF16, 1)
    pTs = Rot(P, "fpT", [128, 8, 128], BF16, 1, psum=True)
    psA = Rot(P, "fpsA", [128, 512], F32, 2, psum=True)
    psB = Rot(P, "fpsB", [128, 512], F32, 2, psum=True)
    psH = Rot(P, "fpsH", [128, 2], F32, 1, psum=True)
    psD = Rot(P, "fpsD", [128, 512], F32, 2, psum=True)
    exts = Rot(P, "fext", [128, 514], F32, 1)
    acs = Rot(P, "fac", [128, 512], F32, 2)
    gT = P.sb("fgT", [128, NJ, 512], BF16); r_gT = P.res()
    outs = Rot(P, "fout", [128, D], F32, 2)
    flip = [0]

    def tr_mod(xn, r_xn, npart, hx, r_hx, cols):
        pT, r_pT = pTs.next()
        for k in range(8):
            P.op("pe", lambda e, k=k, pT=pT: e.transpose(pT[:, k, 0:npart], xn[:npart, k * 128:(k + 1) * 128], G.ident_b[:npart, :npart]),
                 reads=[r_xn, G.r_const], writes=[r_pT])
        for k in range(8):
            flip[0] ^= 1
            if flip[0] and npart == 128:
                P.op("act", lambda e, k=k, pT=pT: e.activation(out=hx[:, k, cols], in_=pT[:, k, 0:npart], func=AF.Identity,
                                                              scale=G.gainF[:, 0, k:k + 1], bias=G.mods[:, 0, 24 + k:25 + k]),
                     reads=[r_pT, G.r_gain, G.r_mods], writes=[r_hx])
            else:
                P.op("dve", lambda e, k=k, pT=pT: e.tensor_scalar(out=hx[:, k, cols], in0=pT[:, k, 0:npart], scalar1=G.gainF[:, 0, k:k + 1],
                                                                 scalar2=G.mods[:, 0, 24 + k:25 + k], op0=ALU.mult, op1=ALU.add),
                     reads=[r_pT, G.r_gain, G.r_mods], writes=[r_hx])

    def front(i):
        t0 = i * 512
        hx, r_hx = hfs.next()
        hl, r_hl = halo.next()
        if i == 0 or i == NCH - 1:
            P.op("pool", lambda e, hl=hl: e.memset(hl[0:2, :], 0.0), writes=[r_hl])
        if i != 0:
            P.dma("sp", hl[0:1, :], io.x2[t0 - 1:t0, :], reads=[G.r_x2], writes=[r_hl])
        if i != NCH - 1:
            P.dma("sp", hl[1:2, :], io.x2[t0 + 512:t0 + 513, :], reads=[G.r_x2], writes=[r_hl])
        hn, r_hn = halon.next()
        norm_tile(P, G, hl, r_hl, 2, hn, r_hn, T)
        tr_mod(hn, r_hn, 2, hx, r_hx, slice(0, 514, 513))
        if i == 0:
            P.op("dve", lambda e, hx=hx: e.memset(hx[:, :, 0:1], 0.0), writes=[r_hx])
        if i == NCH - 1:
            P.op("dve", lambda e, hx=hx: e.memset(hx[:, :, 513:514], 0.0), writes=[r_hx])
        for tt in range(4):
            xt, r_x = xts.next()
            P.dma("sp", xt[:], io.x2[t0 + tt * 128:t0 + (tt + 1) * 128, :], reads=[G.r_x2], writes=[r_x])
            xn, r_xn = xns.next()
            norm_tile(P, G, xt, r_x, 128, xn, r_xn, T)
            tr_mod(xn, r_xn, 128, hx, r_hx, slice(1 + tt * 128, 1 + (tt + 1) * 128))
        return hx, r_hx

    def back(i, hx, r_hx):
        t0 = i * 512
        for j in range(NJ):
            pa, r_pa = psA.next(); ph, r_ph = psH.next(); pb, r_pb = psB.next()
            for k in range(8):
                P.op("pe", lambda e, k=k, j=j, pa=pa, hx=hx: e.matmul(pa[:], wup[:, k, j * 128:(j + 1) * 128], hx[:, k, 1:513],
                                                                     start=(k == 0), stop=(k == 7)), reads=[r_wup, r_hx], writes=[r_pa])
            for k in range(8):
                P.op("pe", lambda e, k=k, j=j, ph=ph, hx=hx: e.matmul(ph[:], wup[:, k, j * 128:(j + 1) * 128], hx[:, k, 0:514:513],
                                                                     start=(k == 0), stop=(k == 7)), reads=[r_wup, r_hx], writes=[r_ph])
            for k in range(8):
                P.op("pe", lambda e, k=k, j=j, pb=pb, hx=hx: e.matmul(pb[:], wup[:, k, DFF + j * 128:DFF + (j + 1) * 128], hx[:, k, 1:513],
                                                                     start=(k == 0), stop=(k == 7)), reads=[r_wup, r_hx], writes=[r_pb])
            ex, r_ex = exts.next()
            P.op("act", lambda e, pa=pa, ex=ex: e.copy(out=ex[:, 1:513], in_=pa[:]), reads=[r_pa], writes=[r_ex])
            P.op("act", lambda e, ph=ph, ex=ex: e.copy(out=ex[:, 0:514:513], in_=ph[:]), reads=[r_ph], writes=[r_ex])
            ac, r_ac = acs.next()
            P.op("dve", lambda e, ex=ex, ac=ac, j=j: e.tensor_scalar(out=ac[:], in0=ex[:, 1:513], scalar1=cw[:, j, 1:2], scalar2=cw[:, j, 3:4],
                                                                    op0=ALU.mult, op1=ALU.add), reads=[r_ex, r_cw], writes=[r_ac])
            P.op("dve", lambda e, ex=ex, ac=ac, j=j: e.scalar_tensor_tensor(out=ac[:], in0=ex[:, 0:512], scalar=cw[:, j, 0:1], in1=ac[:],
                                                                           op0=ALU.mult, op1=ALU.add), reads=[r_ex, r_cw, r_ac], writes=[r_ac])
            P.op("dve", lambda e, ex=ex, ac=ac, j=j: e.scalar_tensor_tensor(out=ac[:], in0=ex[:, 2:514], scalar=cw[:, j, 2:3], in1=ac[:],
                                                                            op0=ALU.mult, op1=ALU.add), reads=[r_ex, r_cw, r_ac], writes=[r_ac])
            P.op("act", lambda e, ac=ac: e.activation(out=ac[:], in_=ac[:], func=AF.Gelu), reads=[r_ac], writes=[r_ac])
            P.op("dve", lambda e, ac=ac, pb=pb, j=j: e.tensor_tensor(out=gT[:, j, :], in0=pb[:], in1=ac[:], op=ALU.mult),
                 reads=[r_pb, r_ac], writes=[r_gT])
        for tt in range(4):
            ot, r_ot = outs.next()
            P.dma("sp", ot[:], io.x2[t0 + tt * 128:t0 + (tt + 1) * 128, :], reads=[G.r_x2], writes=[r_ot])
            for dh in range(2):
                pd, r_pd = psD.next()
                for j in range(NJ):
                    P.op("pe", lambda e, pd=pd, j=j, tt=tt, dh=dh: e.matmul(pd[:], gT[:, j, tt * 128:(tt + 1) * 128],
                                                                          wdn[:, j, dh * 512:(dh + 1) * 512],
                                                                          start=(j == 0), stop=(j == NJ - 1)),
                         reads=[r_gT, r_wdn], writes=[r_pd])
                P.op("dve", lambda e, pd=pd, ot=ot, dh=dh: e.tensor_tensor(out=ot[:, dh * 512:(dh + 1) * 512], in0=pd[:],
                                                                        in1=ot[:, dh * 512:(dh + 1) * 512], op=ALU.add),
                     reads=[r_pd, r_ot], writes=[r_ot])
            junk, r_junk = T["junk"].next(); st, r_st = T["stat"].next()
            P.op("act", lambda e, junk=junk, ot=ot, st=st: e.activation(out=junk[:], in_=ot[:], func=AF.Square, accum_out=st[:, 0:1]),
                 reads=[r_ot], writes=[r_junk, r_st])
            P.op("act", lambda e, st=st: e.activation(out=st[:, 1:2], in_=st[:, 0:1], func=AF.Sqrt, scale=1.0 / D, bias=G.eps[:]),
                 reads=[r_st, G.r_const], writes=[r_st])
            P.op("dve", lambda e, st=st: e.reciprocal(out=st[:, 2:3], in_=st[:, 1:2]), reads=[r_st], writes=[r_st])
            P.op("dve", lambda e, ot=ot, st=st: e.scalar_tensor_tensor(out=ot[:], in0=ot[:], scalar=st[:, 2:3], in1=fnb[:],
                                                                       op0=ALU.mult, op1=ALU.mult), reads=[r_ot, r_st, r_fnb], writes=[r_ot])
            P.dma("sp", io.out[t0 + tt * 128:t0 + (tt + 1) * 128, :], ot[:], reads=[r_ot], writes=[G.r_out])

    nxt = front(0)
    for i in range(nchunks):
        cur = nxt
        if i + 1 < nchunks:
            nxt = front(i + 1)
        back(i, *cur)
```

```python
import math
from contextlib import ExitStack
import numpy as np
import ml_dtypes
import concourse.bass as bass
import concourse.mybir as mybir
from concourse.bass_utils import run_bass_kernel_spmd

F32 = mybir.dt.float32
BF16 = mybir.dt.bfloat16
AF = mybir.ActivationFunctionType
ALU = mybir.AluOpType
AX = mybir.AxisListType

D = 1024
L = 8192
CTX = 256
NKEY = CTX + L
DFF = 2816
EPS = 1e-6
LAM_INIT = 0.2
NCH = 16
TWO_PI = 2.0 * math.pi
NB = 32
NFFT = 16384
PI_LO = 3.1415925


class Res:
    __slots__ = ("name", "w", "r")

    def __init__(self, name):
        self.name = name
        self.w = None
        self.r = {}


class Prog:
    EPOCH = 24000
    NRING = 12

    def __init__(self, nc, es):
        self.nc, self.es = nc, es
        self.eng = {"pe": nc.tensor, "act": nc.scalar, "dve": nc.vector, "pool": nc.gpsimd, "sp": nc.sync}
        self.nsem = 0
        self.ops = {k: [] for k in self.eng}
        self.cnt = {k: 0 for k in self.eng}
        self.cur = {k: self.newsem(k) for k in self.eng}
        self.pesems = {self.cur["pe"]}
        self.waited = {k: {} for k in self.eng}
        self.ring = {q: [[self.newsem(q + "r"), 0] for _ in range(self.NRING)] for q in ("sp", "pool", "act")}
        self.ringpos = {q: 0 for q in self.ring}
        self.nres = 0

    def newsem(self, nm):
        self.nsem += 1
        return self.es.enter_context(self.nc.semaphore(f"s_{nm}_{self.nsem}"))

    def res(self, name=None):
        self.nres += 1
        return Res(name or f"r{self.nres}")

    def sb(self, name, shape, dt, glob=False):
        return (self.es if glob else self.pes).enter_context(self.nc.sbuf_tensor("t_" + name, list(shape), dt))

    def ps(self, name, shape, dt, glob=False):
        return (self.es if glob else self.pes).enter_context(self.nc.psum_tensor("p_" + name, list(shape), dt))

    def barrier(self):
        alltok = []
        for q in self.ring:
            for sem, val in self.ring[q]:
                if val > 0:
                    alltok.append((sem, val))
        for e in ("pe", "act", "dve", "pool"):
            if self.cnt[e] > 0:
                alltok.append((self.cur[e], self.cnt[e]))
        for e in self.eng:
            waits = []
            for s, v in alltok:
                if e != "sp" and s is self.cur[e]:
                    continue
                if self.waited[e].get(s, 0) >= v:
                    continue
                self.waited[e][s] = v
                waits.append((s, v))
            self.ops[e].append((waits, None, None, 0))

    def temp_scope(self):
        prog = self

        class _S:
            def __enter__(self_):
                self_.old = prog.pes
                self_.es = ExitStack()
                self_.es.__enter__()
                prog.pes = self_.es
                return self_

            def __exit__(self_, *a):
                prog.pes = self_.old
                self_.es.__exit__(None, None, None)
                prog.barrier()
                return False
        return _S()

    def end_phase(self):
        self.barrier()
        self.emit()
        self.ops = {k: [] for k in self.eng}

    def _collect(self, e, reads, writes, extra=()):
        toks = {}

        def need(s, v):
            if toks.get(s, 0) < v:
                toks[s] = v

        for r in reads:
            if r.w is not None:
                need(*r.w)
        for w in writes:
            if w.w is not None:
                need(*w.w)
            for s, v in w.r.items():
                need(s, v)
        for s, v in extra:
            need(s, v)
        waits = []
        for s, v in toks.items():
            if e == "pe" and s in self.pesems:
                continue
            if self.waited[e].get(s, 0) >= v:
                continue
            self.waited[e][s] = v
            waits.append((s, v))
        return waits

    def _commit(self, tok, reads, writes):
        s, v = tok
        for r in reads:
            if r.r.get(s, 0) < v:
                r.r[s] = v
        for w in writes:
            w.w = tok
            w.r = {}

    def op(self, e, fn, reads=(), writes=()):
        waits = self._collect(e, reads, writes)
        if self.cnt[e] >= self.EPOCH:
            self.cur[e] = self.newsem(e)
            self.cnt[e] = 0
            if e == "pe":
                self.pesems.add(self.cur[e])
        self.cnt[e] += 1
        tok = (self.cur[e], self.cnt[e])
        self.ops[e].append((waits, fn, tok[0], 1))
        self._commit(tok, reads, writes)
        return tok

    def dma(self, q, out, in_, reads=(), writes=(), **kw):
        slot = self.ring[q][self.ringpos[q] % self.NRING]
        self.ringpos[q] += 1
        sem, val = slot
        extra = [(sem, val)] if val > 0 else []
        waits = self._collect(q, reads, writes, extra)
        slot[1] = val + 16
        tok = (sem, val + 16)
        self.ops[q].append((waits, (lambda e, o=out, i=in_, k=kw: e.dma_start(out=o, in_=i, **k)), sem, 16))
        self._commit(tok, reads, writes)
        return tok

    def emit(self):
        with self.nc.Block() as block:
            def mk(e):
                def body(eng):
                    for waits, fn, sem, inc in self.ops[e]:
                        for s, v in waits:
                            eng.wait_ge(s, v)
                        if fn is not None:
                            fn(eng).then_inc(sem, inc)
                return body
            block.tensor(mk("pe"))
            block.scalar(mk("act"))
            block.vector(mk("dve"))
            block.gpsimd(mk("pool"))
            block.sync(mk("sp"))


class Rot:
    def __init__(self, P, name, shape, dt, n, psum=False):
        self.items = []
        for i in range(n):
            t = (P.ps if psum else P.sb)(f"{name}{i}", shape, dt)
            self.items.append((t, P.res(f"{name}{i}")))
        self.i = 0

    def next(self):
        it = self.items[self.i % len(self.items)]
        self.i += 1
        return it


def rope_tables():
    nf = 16
    inv = (10000.0 ** (-np.arange(nf, dtype=np.float32) / nf)).astype(np.float32)
    t = np.arange(L)
    row = (t // 64).astype(np.float32)
    col = (t % 64).astype(np.float32)
    C = np.zeros((128, L), np.float32)
    S = np.zeros((128, L), np.float32)
    for dp in range(128):
        d = dp % 64
        if d < 32:
            pos, jj, sgn = row, d % 16, (-1.0 if d < 16 else 1.0)
        else:
            pos, jj, sgn = col, (d - 32) % 16, (-1.0 if (d - 32) < 16 else 1.0)
        ang = (pos * inv[jj]).astype(np.float32)
        C[dp] = np.cos(ang)
        S[dp] = sgn * np.sin(ang)
    return C, S


class IO:
    pass


def declare_io(nc, dbg_out=(), dbg_in=()):
    io = IO()

    def inp(name, shape, dt=F32):
        t = nc.dram_tensor(name, list(shape), dt, kind="ExternalInput").ap()
        setattr(io, name, t)
        return t

    def scratch(name, shape, dt):
        kind = "Internal"
        if name in dbg_out:
            kind = "ExternalOutput"
        if name in dbg_in:
            kind = "ExternalInput"
        t = nc.dram_tensor(name, list(shape), dt, kind=kind).ap()
        setattr(io, name, t)
        return t

    inp("x", [L, D]); inp("c", [D]); inp("ctx", [CTX, D]); inp("c_ctx", [D])
    inp("w_mod", [D, 6 * D]); inp("b_mod", [6 * D]); inp("norm_mix", [D]); inp("norm_ffn", [D])
    inp("w_in", [D, 3072])
    inp("lam_q1", [64]); inp("lam_k1", [64]); inp("lam_q2", [64]); inp("lam_k2", [64]); inp("subln", [128])
    inp("hy_conv_w", [3, 1536]); inp("hy_conv_b", [1536])
    inp("hy_w1", [33, 64]); inp("hy_b1", [64]); inp("hy_w2", [64, 64]); inp("hy_b2", [64])
    inp("hy_w3", [64, 64]); inp("hy_b3", [64]); inp("hy_w4", [64, 2048]); inp("hy_freq", [64])
    inp("hy_skip", [2, 512]); inp("hy_norm", [512]); inp("w_out", [D, D])
    inp("ffn_w_up", [D, 2 * DFF]); inp("ffn_conv_w", [3, DFF]); inp("ffn_conv_b", [DFF]); inp("ffn_w_down", [DFF, D])
    inp("final_norm", [D])
    inp("ident_f", [128, 128]); inp("ident_b", [128, 128], BF16)
    inp("rope_c", [128, L]); inp("rope_s", [128, L])
    inp("fft_d1", [128, 260], BF16); inp("fft_g", [128, 65 * 2 * 128], BF16)
    inp("fft_e1", [128, 256], BF16); inp("fft_e2", [128, 256], BF16); inp("fft_h", [65, 128 * 2 * 64], BF16)
    inp("hy_zT", [33, L]); inp("hy_zT2", [33, L]); inp("hy_negd", [128, 16])
    scratch("modv", [2, 6 * D], F32)
    scratch("qT", [512, L], BF16)
    scratch("kT", [512, NKEY], BF16)
    scratch("vv", [NKEY, 512], BF16)
    scratch("uT", [1536, L], F32)
    scratch("hk", [1024, 2 * L], BF16)
    scratch("z2T", [512, L], F32)
    scratch("mixT", [512, L], BF16)
    scratch("x2", [L, D], F32)
    io.out = nc.dram_tensor("out", [L, D], F32, kind="ExternalOutput").ap()
    return io


def phase_mod(P, io, G):
    nc = P.nc
    cs = P.sb("cs", [128, 2, 8], F32); r_cs = P.res()
    P.dma("sp", cs[:, 0, :], io.c.rearrange("(j p) -> p j", p=128), writes=[r_cs], allow_slow_non_contiguous=True)
    P.dma("sp", cs[:, 1, :], io.c_ctx.rearrange("(j p) -> p j", p=128), writes=[r_cs], allow_slow_non_contiguous=True)
    ss = P.sb("ss", [128, 2, 8], F32); r_ss = P.res()
    P.op("act", lambda e: e.activation(out=ss[:], in_=cs[:], func=AF.Silu), reads=[r_cs], writes=[r_ss])
    bm = P.sb("bm", [128, 48], F32); r_bm = P.res()
    P.dma("sp", bm[:], io.b_mod.rearrange("(m p) -> p m", p=128), writes=[r_bm], allow_slow_non_contiguous=True)
    nm = P.sb("nm", [128, 2, 8], F32); r_nm = P.res()
    P.dma("sp", nm[:, 0, :], io.norm_mix.rearrange("(j p) -> p j", p=128), writes=[r_nm], allow_slow_non_contiguous=True)
    P.dma("sp", nm[:, 1, :], io.norm_ffn.rearrange("(j p) -> p j", p=128), writes=[r_nm], allow_slow_non_contiguous=True)

    wm = Rot(P, "wm", [128, 6 * D], F32, 2)
    psm = Rot(P, "psm", [128, 2, 48], F32, 2, psum=True)
    acc = G.mods; r_acc = G.r_mods
    for j in range(8):
        wt, r_w = wm.next()
        for s in range(4):
            q = "sp" if s % 2 == 0 else "act"
            P.dma(q, wt[:, s * 1536:(s + 1) * 1536], io.w_mod[j * 128:(j + 1) * 128, s * 1536:(s + 1) * 1536], writes=[r_w])
        pt, r_p = psm.next()
        for m in range(48):
            P.op("pe", lambda e, m=m, wt=wt, pt=pt, j=j: e.matmul(pt[:, :, m], wt[:, m * 128:(m + 1) * 128], ss[:, :, j],
                                                            start=True, stop=True),
                 reads=[r_w, r_ss], writes=[r_p])
        if j == 0:
            P.op("dve", lambda e, pt=pt: e.tensor_tensor(out=acc[:], in0=pt[:], in1=bm[:].unsqueeze(1).broadcast_to([128, 2, 48]),
                                                     op=ALU.add), reads=[r_p, r_bm], writes=[r_acc])
        else:
            P.op("dve", lambda e, pt=pt: e.tensor_tensor(out=acc[:], in0=pt[:], in1=acc[:], op=ALU.add),
                 reads=[r_p, r_acc], writes=[r_acc])
    ptr = P.ps("modtr", [96, 128], F32); r_ptr = P.res()
    P.op("pe", lambda e: e.transpose(ptr[:], acc[:].rearrange("p w m -> p (w m)"), G.ident_f[:]),
         reads=[r_acc, G.r_const], writes=[r_ptr])
    mrow = P.sb("modrow", [96, 128], F32); r_mrow = P.res()
    P.op("dve", lambda e: e.tensor_copy(mrow[:], ptr[:]), reads=[r_ptr], writes=[r_mrow])
    for w in range(2):
        P.dma("sp", io.modv[w].rearrange("(m p) -> m p", p=128), mrow[w * 48:(w + 1) * 48, :], reads=[r_mrow], writes=[G.r_modv])
    for (gt, sc0, ni) in ((G.gainA, 8, 0), (G.gainF, 32, 1)):
        P.op("dve", lambda e, gt=gt, sc0=sc0, ni=ni: e.scalar_tensor_tensor(
            out=gt[:], in0=acc[:, :, sc0:sc0 + 8], scalar=1.0, in1=nm[:, ni, :].unsqueeze(1).broadcast_to([128, 2, 8]),
            op0=ALU.add, op1=ALU.mult), reads=[r_acc, r_nm], writes=[G.r_gain])


def alloc_globals(P, G):
    G.ident_f = P.sb("ident_f", [128, 128], F32, glob=True)
    G.ident_b = P.sb("ident_b", [128, 128], BF16, glob=True)
    G.eps = P.sb("epsc", [128, 1], F32, glob=True)
    G.mods = P.sb("modacc", [128, 2, 48], F32, glob=True)
    G.gainA = P.sb("gainA", [128, 2, 8], F32, glob=True)
    G.gainF = P.sb("gainF", [128, 2, 8], F32, glob=True)
    G.r_const = P.res("const"); G.r_mods = P.res("mods"); G.r_gain = P.res("gain")


def load_consts(P, io, G):
    P.dma("sp", G.ident_f[:], io.ident_f[:, :], writes=[G.r_const])
    P.dma("sp", G.ident_b[:], io.ident_b[:, :], writes=[G.r_const])
    P.op("dve", lambda e: e.memset(G.eps[:], EPS), writes=[G.r_const])


def norm_tile(P, G, xt, r_x, npart, xn, r_xn, T):
    junk, r_junk = T["junk"].next()
    st, r_st = T["stat"].next()
    P.op("act", lambda e: e.activation(out=junk[:npart, :], in_=xt[:npart, :], func=AF.Square, accum_out=st[:npart, 0:1]),
         reads=[r_x], writes=[r_junk, r_st])
    P.op("act", lambda e: e.activation(out=st[:npart, 1:2], in_=st[:npart, 0:1], func=AF.Sqrt, scale=1.0 / D, bias=G.eps[:npart, :]),
         reads=[r_st, G.r_const], writes=[r_st])
    P.op("dve", lambda e: e.reciprocal(out=st[:npart, 2:3], in_=st[:npart, 1:2]), reads=[r_st], writes=[r_st])
    P.op("dve", lambda e: e.tensor_scalar(out=xn[:npart, :], in0=xt[:npart, :], scalar1=st[:npart, 2:3], scalar2=None,
                                          op0=ALU.mult), reads=[r_x, r_st], writes=[r_xn])


def phase_proj(P, io, G, nchunks=NCH):
    nc = P.nc
    wbf = P.sb("wbf", [128, 8, 3072], BF16); r_wbf = P.res()
    wpm = P.sb("wpm", [128, 8, 1024], BF16); r_wpm = P.res()
    wst = Rot(P, "wst", [128, 3072], F32, 2)
    for k in range(8):
        st, r_st = wst.next()
        for s in range(2):
            P.dma("sp" if s == 0 else "act", st[:, s * 1536:(s + 1) * 1536], io.w_in[k * 128:(k + 1) * 128, s * 1536:(s + 1) * 1536],
                  writes=[r_st])
        P.op("pool", lambda e, st=st, k=k: e.tensor_copy(wbf[:, k, :], st[:]), reads=[r_st], writes=[r_wbf])
        sv = st[:, 0:1024].rearrange("p (g t s) -> p g t s", t=2, s=16)
        dv = wpm[:, k, :].rearrange("p (g t s) -> p g t s", t=2, s=16)
        P.op("dve", lambda e, sv=sv, dv=dv: e.tensor_copy(dv[:, :, 0, :], sv[:, :, 1, :]), reads=[r_st], writes=[r_wpm])
        P.op("dve", lambda e, sv=sv, dv=dv: e.tensor_copy(dv[:, :, 1, :], sv[:, :, 0, :]), reads=[r_st], writes=[r_wpm])
    cw = P.sb("hycw", [128, 12, 4], F32); r_cw = P.res()
    for i in range(3):
        P.dma("sp", cw[:, :, i], io.hy_conv_w[i].rearrange("(j p) -> p j", p=128), writes=[r_cw], allow_slow_non_contiguous=True)
    P.dma("sp", cw[:, :, 3], io.hy_conv_b.rearrange("(j p) -> p j", p=128), writes=[r_cw], allow_slow_non_contiguous=True)

    T = {"junk": Rot(P, "junk", [128, D], F32, 1), "stat": Rot(P, "stat", [128, 4], F32, 4)}
    xts = Rot(P, "xt", [128, D], F32, 3)
    xns = Rot(P, "xn", [128, D], BF16, 2)
    hxs = Rot(P, "hxT", [128, 8, 514], BF16, 2)
    pTs = Rot(P, "pT", [128, 8, 128], BF16, 2, psum=True)
    psA = Rot(P, "psA", [128, 512], F32, 2, psum=True)
    psB = Rot(P, "psB", [128, 512], F32, 2, psum=True)
    psH = Rot(P, "psH", [128, 2], F32, 1, psum=True)
    ropec = Rot(P, "ropec", [128, 512], F32, 2)
    ropes = Rot(P, "ropes", [128, 512], F32, 2)
    t1s = Rot(P, "t1s", [128, 512], F32, 2)
    t2s = Rot(P, "t2s", [128, 512], F32, 2)
    qks = Rot(P, "qks", [128, 512], BF16, 3)
    vst = Rot(P, "vst", [128, 512], BF16, 3)
    exts = Rot(P, "ext", [128, 514], F32, 2)
    ust = Rot(P, "ust", [128, 512], F32, 3)
    halo = Rot(P, "halo", [2, D], F32, 2)
    halon = Rot(P, "halon", [2, D], BF16, 2)
    flip = [0]

    def transposed_modulated(xn, r_xn, npart, hx, r_hx, col0, w):
        pT, r_pT = pTs.next()
        for k in range(8):
            P.op("pe", lambda e, k=k, pT=pT: e.transpose(pT[:, k, 0:npart], xn[:npart, k * 128:(k + 1) * 128],
                                                         G.ident_b[:npart, :npart]),
                 reads=[r_xn, G.r_const], writes=[r_pT])
        for k in range(8):
            flip[0] ^= 1
            if flip[0]:
                P.op("act", lambda e, k=k, pT=pT: e.activation(out=hx[:, k, col0:col0 + npart], in_=pT[:, k, 0:npart],
                                                              func=AF.Identity, scale=G.gainA[:, w, k:k + 1],
                                                              bias=G.mods[:, w, k:k + 1]),
                     reads=[r_pT, G.r_gain, G.r_mods], writes=[r_hx])
            else:
                P.op("dve", lambda e, k=k, pT=pT: e.tensor_scalar(out=hx[:, k, col0:col0 + npart], in0=pT[:, k, 0:npart],
                                                                 scalar1=G.gainA[:, w, k:k + 1], scalar2=G.mods[:, w, k:k + 1],
                                                                 op0=ALU.mult, op1=ALU.add),
                     reads=[r_pT, G.r_gain, G.r_mods], writes=[r_hx])

    hc, r_hc = hxs.next()
    for tt in range(2):
        xt, r_x = xts.next()
        P.dma("sp", xt[:], io.ctx[tt * 128:(tt + 1) * 128, :], writes=[r_x])
        xn, r_xn = xns.next()
        norm_tile(P, G, xt, r_x, 128, xn, r_xn, T)
        transposed_modulated(xn, r_xn, 128, hc, r_hc, 1 + tt * 128, 1)
    for cc in range(4, 8):
        pa, r_pa = psA.next()
        for k in range(8):
            P.op("pe", lambda e, k=k, cc=cc, pa=pa: e.matmul(pa[:, 0:256], wbf[:, k, cc * 128:(cc + 1) * 128], hc[:, k, 1:257],
                                                            start=(k == 0), stop=(k == 7)), reads=[r_wbf, r_hc], writes=[r_pa])
        qk, r_qk = qks.next()
        P.op("act", lambda e, pa=pa, qk=qk: e.copy(out=qk[:, 0:256], in_=pa[:, 0:256]), reads=[r_pa], writes=[r_qk])
        P.dma("sp", io.kT[(cc - 4) * 128:(cc - 3) * 128, 0:256], qk[:, 0:256], reads=[r_qk], writes=[G.r_kT])
    for tt in range(2):
        pa, r_pa = psA.next()
        for k in range(8):
            P.op("pe", lambda e, k=k, tt=tt, pa=pa: e.matmul(pa[:], hc[:, k, 1 + tt * 128:1 + (tt + 1) * 128], wbf[:, k, 1024:1536],
                                                            start=(k == 0), stop=(k == 7)), reads=[r_wbf, r_hc], writes=[r_pa])
        vs, r_vs = vst.next()
        P.op("act", lambda e, pa=pa, vs=vs: e.copy(out=vs[:], in_=pa[:]), reads=[r_pa], writes=[r_vs])
        P.dma("sp", io.vv[tt * 128:(tt + 1) * 128, :], vs[:], reads=[r_vs], writes=[G.r_vv])

    def front(i):
        t0 = i * 512
        hx, r_hx = hxs.next()
        hl, r_hl = halo.next()
        if i == 0 or i == NCH - 1:
            P.op("pool", lambda e, hl=hl: e.memset(hl[0:2, :], 0.0), writes=[r_hl])
        if i != 0:
            P.dma("sp", hl[0:1, :], io.x[t0 - 1:t0, :], writes=[r_hl])
        if i != NCH - 1:
            P.dma("sp", hl[1:2, :], io.x[t0 + 512:t0 + 513, :], writes=[r_hl])
        hn, r_hn = halon.next()
        norm_tile(P, G, hl, r_hl, 2, hn, r_hn, T)
        pT, r_pT = pTs.next()
        for k in range(8):
            P.op("pe", lambda e, k=k, pT=pT, hn=hn: e.transpose(pT[:, k, 0:2], hn[0:2, k * 128:(k + 1) * 128], G.ident_b[0:2, 0:2]),
                 reads=[r_hn, G.r_const], writes=[r_pT])
        for k in range(8):
            P.op("dve", lambda e, k=k, pT=pT, hx=hx: e.tensor_scalar(
                out=hx[:, k, 0:514:513], in0=pT[:, k, 0:2], scalar1=G.gainA[:, 0, k:k + 1], scalar2=G.mods[:, 0, k:k + 1],
                op0=ALU.mult, op1=ALU.add), reads=[r_pT, G.r_gain, G.r_mods], writes=[r_hx])
        if i == 0:
            P.op("dve", lambda e, hx=hx: e.memset(hx[:, :, 0:1], 0.0), writes=[r_hx])
        if i == NCH - 1:
            P.op("dve", lambda e, hx=hx: e.memset(hx[:, :, 513:514], 0.0), writes=[r_hx])
        for tt in range(4):
            xt, r_x = xts.next()
            P.dma("sp", xt[:], io.x[t0 + tt * 128:t0 + (tt + 1) * 128, :], writes=[r_x])
            xn, r_xn = xns.next()
            norm_tile(P, G, xt, r_x, 128, xn, r_xn, T)
            transposed_modulated(xn, r_xn, 128, hx, r_hx, 1 + tt * 128, 0)
        return hx, r_hx

    def back(i, hx, r_hx):
        t0 = i * 512
        rc, r_rc = ropec.next(); rs, r_rs = ropes.next()
        P.dma("sp", rc[:], io.rope_c[:, t0:t0 + 512], writes=[r_rc])
        P.dma("sp", rs[:], io.rope_s[:, t0:t0 + 512], writes=[r_rs])
        for cc in range(8):
            pa, r_pa = psA.next(); pb, r_pb = psB.next()
            for k in range(8):
                P.op("pe", lambda e, k=k, cc=cc, pa=pa, hx=hx: e.matmul(pa[:], wbf[:, k, cc * 128:(cc + 1) * 128], hx[:, k, 1:513],
                                                                       start=(k == 0), stop=(k == 7)),
                     reads=[r_wbf, r_hx], writes=[r_pa])
            for k in range(8):
                P.op("pe", lambda e, k=k, cc=cc, pb=pb, hx=hx: e.matmul(pb[:], wpm[:, k, cc * 128:(cc + 1) * 128], hx[:, k, 1:513],
                                                                       start=(k == 0), stop=(k == 7)),
                     reads=[r_wpm, r_hx], writes=[r_pb])
            t1, r_t1 = t1s.next(); t2, r_t2 = t2s.next()
            P.op("dve", lambda e, pa=pa, t1=t1, rc=rc: e.tensor_tensor(out=t1[:], in0=pa[:], in1=rc[:], op=ALU.mult),
                 reads=[r_pa, r_rc], writes=[r_t1])
            P.op("dve", lambda e, pb=pb, t2=t2, rs=rs: e.tensor_tensor(out=t2[:], in0=pb[:], in1=rs[:], op=ALU.mult),
                 reads=[r_pb, r_rs], writes=[r_t2])
            qk, r_qk = qks.next()
            P.op("dve", lambda e, t1=t1, t2=t2, qk=qk: e.tensor_tensor(out=qk[:], in0=t1[:], in1=t2[:], op=ALU.add),
                 reads=[r_t1, r_t2], writes=[r_qk])
            if cc < 4:
                P.dma("sp", io.qT[cc * 128:(cc + 1) * 128, t0:t0 + 512], qk[:], reads=[r_qk], writes=[G.r_qT])
            else:
                P.dma("sp", io.kT[(cc - 4) * 128:(cc - 3) * 128, CTX + t0:CTX + t0 + 512], qk[:], reads=[r_qk], writes=[G.r_kT])
        for tt in range(4):
            pa, r_pa = psA.next()
            for k in range(8):
                P.op("pe", lambda e, k=k, tt=tt, pa=pa, hx=hx: e.matmul(pa[:], hx[:, k, 1 + tt * 128:1 + (tt + 1) * 128],
                                                                       wbf[:, k, 1024:1536], start=(k == 0), stop=(k == 7)),
                     reads=[r_wbf, r_hx], writes=[r_pa])
            vs, r_vs = vst.next()
            P.op("act", lambda e, pa=pa, vs=vs: e.copy(out=vs[:], in_=pa[:]), reads=[r_pa], writes=[r_vs])
            P.dma("sp", io.vv[CTX + t0 + tt * 128:CTX + t0 + (tt + 1) * 128, :], vs[:], reads=[r_vs], writes=[G.r_vv])
        for cc in range(12):
            c0 = 1536 + cc * 128
            pb, r_pb = psB.next(); ph, r_ph = psH.next()
            for k in range(8):
                P.op("pe", lambda e, k=k, c0=c0, pb=pb, hx=hx: e.matmul(pb[:], wbf[:, k, c0:c0 + 128], hx[:, k, 1:513],
                                                                       start=(k == 0), stop=(k == 7)),
                     reads=[r_wbf, r_hx], writes=[r_pb])
            for k in range(8):
                P.op("pe", lambda e, k=k, c0=c0, ph=ph, hx=hx: e.matmul(ph[:], wbf[:, k, c0:c0 + 128], hx[:, k, 0:514:513],
                                                                       start=(k == 0), stop=(k == 7)),
                     reads=[r_wbf, r_hx], writes=[r_ph])
            ex, r_ex = exts.next()
            P.op("act", lambda e, pb=pb, ex=ex: e.copy(out=ex[:, 1:513], in_=pb[:]), reads=[r_pb], writes=[r_ex])
            P.op("act", lambda e, ph=ph, ex=ex: e.copy(out=ex[:, 0:514:513], in_=ph[:]), reads=[r_ph], writes=[r_ex])
            us, r_us = ust.next()
            P.op("dve", lambda e, ex=ex, us=us, cc=cc: e.tensor_scalar(out=us[:], in0=ex[:, 1:513], scalar1=cw[:, cc, 1:2],
                                                                      scalar2=cw[:, cc, 3:4], op0=ALU.mult, op1=ALU.add),
                 reads=[r_ex, r_cw], writes=[r_us])
            P.op("dve", lambda e, ex=ex, us=us, cc=cc: e.scalar_tensor_tensor(out=us[:], in0=ex[:, 0:512], scalar=cw[:, cc, 0:1],
                                                                             in1=us[:], op0=ALU.mult, op1=ALU.add),
                 reads=[r_ex, r_cw, r_us], writes=[r_us])
            P.op("dve", lambda e, ex=ex, us=us, cc=cc: e.scalar_tensor_tensor(out=us[:], in0=ex[:, 2:514], scalar=cw[:, cc, 2:3],
                                                                             in1=us[:], op0=ALU.mult, op1=ALU.add),
                 reads=[r_ex, r_cw, r_us], writes=[r_us])
            P.dma("sp", io.uT[cc * 128:(cc + 1) * 128, t0:t0 + 512], us[:], reads=[r_us], writes=[G.r_uT])

    nxt = front(0)
    for i in range(nchunks):
        cur = nxt
        if i + 1 < nchunks:
            nxt = front(i + 1)
        back(i, *cur)


def build_program(dbg_out=(), dbg_in=(), phases=("mod", "proj", "filt", "hyena", "attn", "outproj", "ffn"), nchunks=NCH, nblocks=512 // NB, nheads=4, nqc=NCH):
    nc = bass.Bass("TRN2", target_bir_lowering=False)
    io = declare_io(nc, dbg_out, dbg_in)
    with ExitStack() as es:
        P = Prog(nc, es)
        G = IO()
        alloc_globals(P, G)
        plist = [("consts", lambda: load_consts(P, io, G)),
                 ("mod", lambda: phase_mod(P, io, G)),
                 ("proj", lambda: phase_proj(P, io, G, nchunks)),
                 ("filt", lambda: phase_filters(P, io, G)),
                 ("hyena", lambda: phase_hyena(P, io, G, nblocks)),
                 ("attn", lambda: phase_attn(P, io, G, nheads, nqc)),
                 ("outproj", lambda: phase_outproj(P, io, G, nchunks)),
                 ("ffn", lambda: phase_ffn(P, io, G, nchunks))]
        for nm in ("r_uT", "r_hk", "r_qT", "r_kT", "r_vv", "r_z2T", "r_modv", "r_mixT", "r_x2", "r_out"):
            setattr(G, nm, P.res(nm))
        for name, fn in plist:
            if name not in phases and name != "consts":
                continue
            with ExitStack() as pes:
                P.pes = pes
                fn()
                P.end_phase()
    return nc


def host_consts():
    C, S = rope_tables()
    return {
        "ident_f": np.eye(128, dtype=np.float32),
        "ident_b": np.eye(128, dtype=np.float32).astype(ml_dtypes.bfloat16),
        "rope_c": C, "rope_s": S, **fft_tables(), **hyena_consts(),
    }


def make_in_maps(inputs):
    consts = host_consts()
    shared = {}
    for k, v in inputs.items():
        if k in ("x", "c", "ctx"):
            continue
        a = np.asarray(v)
        if k not in ("c_ctx", "final_norm"):
            a = a[0]
        shared[k] = np.ascontiguousarray(a)
    shared.update(consts)
    maps = []
    for b in range(8):
        m = dict(shared)
        m["x"] = np.ascontiguousarray(np.asarray(inputs["x"])[b])
        m["c"] = np.ascontiguousarray(np.asarray(inputs["c"])[b])
        m["ctx"] = np.ascontiguousarray(np.asarray(inputs["ctx"])[b])
        maps.append(m)
    return maps


def kernel(**inputs):
    nc = build_program()
    maps = make_in_maps(inputs)
    res = run_bass_kernel_spmd(nc, maps, core_ids=list(range(8)))
    return np.stack([r["out"] for r in res.results], axis=0)


def fft_tables():
    N = NFFT
    a = np.arange(128)[:, None]; k1 = np.arange(65)[None, :]
    psi = 2 * np.pi * ((a * k1) % 128) / 128
    D1 = np.stack([np.cos(psi), -np.sin(psi), np.sin(psi), np.cos(psi)], 1).reshape(128, 260)
    b = np.arange(128)[:, None, None]; k1 = np.arange(65)[None, :, None]; k2 = np.arange(128)[None, None, :]
    phi = 2 * np.pi * ((b * (k1 + 128 * k2)) % N) / N
    G = np.stack([np.cos(phi), -np.sin(phi)], 2).reshape(128, 65 * 2 * 128)
    k2 = np.arange(128)[:, None]; bb = np.arange(128)[None, :]
    th = 2 * np.pi * ((k2 * bb) % 128) / 128
    E1 = np.stack([np.cos(th), np.sin(th)], 1).reshape(128, 256)
    E2 = np.stack([-np.sin(th), np.cos(th)], 1).reshape(128, 256)
    k1 = np.arange(65)[:, None, None]; b = np.arange(128)[None, :, None]; a = np.arange(64)[None, None, :]
    chi = 2 * np.pi * (((128 * a + b) * k1) % N) / N
    ck = np.where((k1 == 0) | (k1 == 64), 1.0, 2.0) / N
    H = np.stack([ck * np.cos(chi), -ck * np.sin(chi)], 2).reshape(65, 128 * 2 * 64)
    bf = ml_dtypes.bfloat16
    return {"fft_d1": D1.astype(np.float32).astype(bf), "fft_g": G.astype(np.float32).astype(bf),
            "fft_e1": E1.astype(np.float32).astype(bf), "fft_e2": E2.astype(np.float32).astype(bf),
            "fft_h": H.astype(np.float32).astype(bf)}


def hyena_consts():
    t = np.linspace(0.0, 1.0, L, dtype=np.float32)[:, None]
    w = (2.0 * np.float32(math.pi) * np.arange(L, dtype=np.float32)[:, None] / np.float32(L)).astype(np.float32)
    f = np.linspace(1e-4, 15, 16, dtype=np.float32)[None, :]
    z = np.concatenate([t, np.cos(f * w), -np.sin(f * w)], axis=-1).astype(np.float32)
    min_decay = math.log(1e-2) / 0.3
    max_decay = math.log(1e-2) / 1.5
    deltas = np.abs(np.linspace(min_decay, max_decay, 2048, dtype=np.float32))
    negd = (-deltas).reshape(16, 128).T.copy()
    zT = np.ascontiguousarray(z.T)
    zT2 = zT.copy()
    zT2[:, 1:] = zT[:, :0:-1]
    return {"hy_zT": zT, "hy_zT2": np.ascontiguousarray(zT2), "hy_negd": negd.astype(np.float32)}


def phase_filters(P, io, G):
    w1 = P.sb("fw1", [33, 64], F32); w2 = P.sb("fw2", [64, 64], F32); w3 = P.sb("fw3", [64, 64], F32)
    w4 = P.sb("fw4", [64, 2048], F32); r_w = P.res()
    P.dma("sp", w1[:], io.hy_w1[:, :], writes=[r_w]); P.dma("sp", w2[:], io.hy_w2[:, :], writes=[r_w])
    P.dma("sp", w3[:], io.hy_w3[:, :], writes=[r_w]); P.dma("act", w4[:], io.hy_w4[:, :], writes=[r_w])
    fr = P.sb("ffr", [64, 8], F32); r_fr = P.res()
    P.dma("sp", fr[:, 0:1], io.hy_freq.rearrange("(p o) -> p o", o=1), writes=[r_fr])
    for i, bb in enumerate((io.hy_b1, io.hy_b2, io.hy_b3)):
        P.dma("sp", fr[:, 1 + i:2 + i], bb.rearrange("(p o) -> p o", o=1), writes=[r_fr])
    P.op("dve", lambda e: e.tensor_scalar(out=fr[:, 4:7], in0=fr[:, 1:4], scalar1=fr[:, 0:1], scalar2=None, op0=ALU.mult),
         reads=[r_fr], writes=[r_fr])
    negd = P.sb("fnegd", [128, 16], F32); r_nd = P.res()
    P.dma("sp", negd[:], io.hy_negd[:, :], writes=[r_nd])
    zts = Rot(P, "fzT", [33, 512], F32, 2)
    tnb = Rot(P, "ftn", [128, 512], F32, 2)
    hs = [Rot(P, f"fh{i}", [64, 512], F32, 2) for i in range(3)]
    pre = Rot(P, "fpre", [64, 512], F32, 2); qi = Rot(P, "fqi", [64, 512], mybir.dt.int32, 2)
    qf = Rot(P, "fqf", [64, 512], F32, 2)
    psl = Rot(P, "fps", [64, 512], F32, 2, psum=True)
    ps4 = Rot(P, "fps4", [128, 512], F32, 3, psum=True)
    wins = Rot(P, "fwin", [128, 512], F32, 3)
    hst = Rot(P, "fhst", [128, 512], BF16, 4)
    ws = [w1, w2, w3]
    for i in range(NCH):
      for d in range(2):
        t0 = i * 512
        src = io.hy_zT if d == 0 else io.hy_zT2
        zt, r_zt = zts.next()
        P.dma("sp", zt[:], src[:, t0:t0 + 512], writes=[r_zt])
        tn, r_tn = tnb.next()
        P.dma("sp", tn[:], src[0:1, t0:t0 + 512].partition_broadcast(128), writes=[r_tn])
        cur, r_cur = zt, r_zt
        for l in range(3):
            ps, r_ps = psl.next()
            kk = 33 if l == 0 else 64
            P.op("pe", lambda e, ps=ps, l=l, cur=cur, kk=kk: e.matmul(ps[:], ws[l][:kk, :], cur[:kk, :], start=True, stop=True),
                 reads=[r_w, r_cur], writes=[r_ps])
            pr, r_pr = pre.next(); q1, r_q1 = qi.next(); q2, r_q2 = qf.next()
            P.op("dve", lambda e, ps=ps, pr=pr, l=l: e.tensor_scalar(out=pr[:], in0=ps[:], scalar1=fr[:, 0:1], scalar2=fr[:, 4 + l:5 + l],
                                                                  op0=ALU.mult, op1=ALU.add), reads=[r_ps, r_fr], writes=[r_pr])
            P.op("dve", lambda e, pr=pr, q1=q1: e.tensor_scalar(out=q1[:], in0=pr[:], scalar1=1.0 / TWO_PI, scalar2=None, op0=ALU.mult),
                 reads=[r_pr], writes=[r_q1])
            P.op("dve", lambda e, q1=q1, q2=q2: e.tensor_copy(q2[:], q1[:]), reads=[r_q1], writes=[r_q2])
            P.op("dve", lambda e, q2=q2, pr=pr: e.scalar_tensor_tensor(out=pr[:], in0=q2[:], scalar=-TWO_PI, in1=pr[:],
                                                                       op0=ALU.mult, op1=ALU.add), reads=[r_q2, r_pr], writes=[r_pr])
            P.op("dve", lambda e, pr=pr: e.tensor_scalar(out=pr[:], in0=pr[:], scalar1=PI_LO, scalar2=-PI_LO, op0=ALU.min, op1=ALU.max),
                 reads=[r_pr], writes=[r_pr])
            h, r_h = hs[l].next()
            P.op("act", lambda e, pr=pr, h=h: e.activation(out=h[:], in_=pr[:], func=AF.Sin), reads=[r_pr], writes=[r_h])
            cur, r_cur = h, r_h
        for j in range(16):
            if (j // 4) % 2 != d:
                continue
            n_, cc = j // 8, j % 4
            ps, r_ps = ps4.next()
            P.op("pe", lambda e, ps=ps, j=j, cur=cur: e.matmul(ps[:], w4[:, j * 128:(j + 1) * 128], cur[:], start=True, stop=True),
                 reads=[r_w, r_cur], writes=[r_ps])
            wn, r_wn = wins.next()
            P.op("act", lambda e, wn=wn, tn=tn, j=j: e.activation(out=wn[:], in_=tn[:], func=AF.Exp, scale=negd[:, j:j + 1]),
                 reads=[r_tn, r_nd], writes=[r_wn])
            st, r_st = hst.next()
            P.op("dve", lambda e, ps=ps, wn=wn, st=st: e.tensor_tensor(out=st[:], in0=ps[:], in1=wn[:], op=ALU.mult),
                 reads=[r_ps, r_wn], writes=[r_st])
            if i == 0 and d == 1:
                P.op("dve", lambda e, st=st: e.memset(st[:, 0:1], 0.0), writes=[r_st])
            r0 = n_ * 512 + cc * 128
            P.dma("sp", io.hk[r0:r0 + 128, d * L + t0:d * L + t0 + 512], st[:], reads=[r_st], writes=[G.r_hk])


def phase_hyena(P, io, G, nblocks=512 // NB):
    d1 = P.sb("d1", [128, 260], BF16); gt = P.sb("gt", [128, 65, 2, 128], BF16)
    e1 = P.sb("e1", [128, 256], BF16); e2 = P.sb("e2", [128, 256], BF16)
    ht = P.sb("ht", [65, 128, 2, 64], BF16); r_tab = P.res()
    P.dma("sp", d1[:], io.fft_d1[:, :], writes=[r_tab])
    for s in range(5):
        ks = slice(s * 13, (s + 1) * 13)
        P.dma("sp" if s % 2 == 0 else "act", gt[:, ks], io.fft_g.rearrange("p (k r m) -> p k r m", r=2, m=128)[:, ks], writes=[r_tab])
    P.dma("sp", e1[:], io.fft_e1[:, :], writes=[r_tab]); P.dma("sp", e2[:], io.fft_e2[:, :], writes=[r_tab])
    for s in range(4):
        bs = slice(s * 32, (s + 1) * 32)
        P.dma("act" if s % 2 == 0 else "sp", ht[:, bs], io.fft_h.rearrange("p (b r a) -> p b r a", r=2, a=64)[:, bs], writes=[r_tab])
    zins = Rot(P, "zin", [128, NB, 128], BF16, 2)
    tA = P.sb("gA", [64, NB, 128], F32); r_A = P.res()
    tB = P.sb("gB", [64, NB, 128], F32); r_B = P.res()
    Ys = P.sb("Ys", [128, NB, 4, 65], BF16); r_Ys = P.res()
    Kf = [P.sb(f"Kf{n}", [128, 65, 2, NB], F32) for n in range(2)]; r_Kf = [P.res(), P.res()]
    Ps = P.sb("Ps", [128, 2, NB, 65], BF16); r_Ps = P.res()
    U = P.sb("U", [65, NB, 2, 128], BF16); r_U = P.res()
    tmps = [Rot(P, f"ctmp{i}", [128, 8, NB], F32, 2) for i in range(4)]
    gtmp = Rot(P, "gtmp", [64, NB, 16], F32, 2)
    skb = P.sb("skb", [64, 2, NB], F32); r_skb = P.res()
    ps1 = Rot(P, "hps1", [128, 260], F32, 2, psum=True)
    ps2 = Rot(P, "hps2", [128, 8, 2, NB], F32, 2, psum=True)
    psa = Rot(P, "hpsa", [65, 256], F32, 2, psum=True)
    psb = Rot(P, "hpsb", [64, 16, NB], F32, 2, psum=True)
    flip = [0]

    def sig_ap(src, r0):
        return src[r0:r0 + NB, :].rearrange("c (a b) -> a c b", b=128)

    def fwd(zin, r_zin, consumer, kp=64):
        for c in range(NB):
            ps, r_ps = ps1.next()
            P.op("pe", lambda e, ps=ps, c=c: e.matmul(ps[:], zin[:kp, c, :], d1[:kp, :], start=True, stop=True),
                 reads=[r_zin, r_tab], writes=[r_ps])
            flip[0] ^= 1
            dst = Ys[:, c, :, :]
            src = ps[:].rearrange("p (q k) -> p q k", q=4)
            if flip[0]:
                P.op("act", lambda e, dst=dst, src=src: e.copy(out=dst, in_=src), reads=[r_ps], writes=[r_Ys])
            else:
                P.op("dve", lambda e, dst=dst, src=src: e.tensor_copy(dst, src), reads=[r_ps], writes=[r_Ys])
        for g in range(9):
            k1s = list(range(g * 8, min(65, g * 8 + 8)))
            ps, r_ps = ps2.next()
            for j, k1 in enumerate(k1s):
                P.op("pe", lambda e, ps=ps, j=j, k1=k1: e.matmul(ps[:, j], gt[:, k1, 0, :], Ys[:, :, 0:2, k1].rearrange("p c q -> p q c"), start=True, stop=False),
                     reads=[r_tab, r_Ys], writes=[r_ps])
                P.op("pe", lambda e, ps=ps, j=j, k1=k1: e.matmul(ps[:, j], gt[:, k1, 1, :], Ys[:, :, 2:4, k1].rearrange("p c q -> p q c"), start=False, stop=True),
                     reads=[r_tab, r_Ys], writes=[r_ps])
            consumer(g, k1s, ps, r_ps)

    def filt_first(n):
        def cons(g, k1s, ps, r_ps):
            nk = len(k1s)
            P.op("act", lambda e: e.copy(out=Kf[n][:, k1s[0]:k1s[0] + nk], in_=ps[:, 0:nk]), reads=[r_ps], writes=[r_Kf[n]])
        return cons

    def filt_second(n):
        def cons(g, k1s, ps, r_ps):
            nk = len(k1s); ks = slice(k1s[0], k1s[0] + nk)
            P.op("dve", lambda e: e.tensor_tensor(out=Kf[n][:, ks, 0, :], in0=Kf[n][:, ks, 0, :], in1=ps[:, 0:nk, 0, :], op=ALU.add),
                 reads=[r_ps, r_Kf[n]], writes=[r_Kf[n]])
            P.op("dve", lambda e: e.tensor_tensor(out=Kf[n][:, ks, 1, :], in0=Kf[n][:, ks, 1, :], in1=ps[:, 0:nk, 1, :], op=ALU.subtract),
                 reads=[r_ps, r_Kf[n]], writes=[r_Kf[n]])
        return cons

    def data_mul(n):
        def cons(g, k1s, ps, r_ps):
            nk = len(k1s); ks = slice(k1s[0], k1s[0] + nk)
            t = [tm.next() for tm in tmps]
            combos = [(0, 0), (1, 1), (0, 1), (1, 0)]
            for i, (xp, kp) in enumerate(combos):
                P.op("dve", lambda e, i=i, xp=xp, kp=kp: e.tensor_tensor(out=t[i][0][:, 0:nk, :], in0=ps[:, 0:nk, xp, :],
                                                                         in1=Kf[n][:, ks, kp, :], op=ALU.mult),
                     reads=[r_ps, r_Kf[n]], writes=[t[i][1]])
            P.op("pool", lambda e: e.tensor_tensor(out=Ps[:, 0, :, ks].rearrange("p c k -> p k c"), in0=t[0][0][:, 0:nk, :],
                                                   in1=t[1][0][:, 0:nk, :], op=ALU.subtract),
                 reads=[t[0][1], t[1][1]], writes=[r_Ps])
            P.op("pool", lambda e: e.tensor_tensor(out=Ps[:, 1, :, ks].rearrange("p c k -> p k c"), in0=t[2][0][:, 0:nk, :],
                                                   in1=t[3][0][:, 0:nk, :], op=ALU.add),
                 reads=[t[2][1], t[3][1]], writes=[r_Ps])
        return cons

    def inverse(gate):
        for c in range(NB):
            ps, r_ps = psa.next()
            P.op("pe", lambda e, ps=ps, c=c: e.matmul(ps[:], Ps[:, 0, c, :], e1[:], start=True, stop=False),
                 reads=[r_Ps, r_tab], writes=[r_ps])
            P.op("pe", lambda e, ps=ps, c=c: e.matmul(ps[:], Ps[:, 1, c, :], e2[:], start=False, stop=True),
                 reads=[r_Ps, r_tab], writes=[r_ps])
            flip[0] ^= 1
            dst = U[:, c, :, :]
            src = ps[:].rearrange("p (r b) -> p r b", r=2)
            if flip[0]:
                P.op("act", lambda e, dst=dst, src=src: e.copy(out=dst, in_=src), reads=[r_ps], writes=[r_U])
            else:
                P.op("dve", lambda e, dst=dst, src=src: e.tensor_copy(dst, src), reads=[r_ps], writes=[r_U])
        for g in range(8):
            ps, r_ps = psb.next()
            for j in range(16):
                b = g * 16 + j
                P.op("pe", lambda e, ps=ps, j=j, b=b: e.matmul(ps[:, j, :], ht[:, b, 0, :], U[:, :, 0, b], start=True, stop=False),
                     reads=[r_tab, r_U], writes=[r_ps])
                P.op("pe", lambda e, ps=ps, j=j, b=b: e.matmul(ps[:, j, :], ht[:, b, 1, :], U[:, :, 1, b], start=False, stop=True),
                     reads=[r_tab, r_U], writes=[r_ps])
            gate(g, ps, r_ps)

    for blk in range(nblocks):
        c0 = blk * NB
        P.dma("sp", skb[:, 0, :], io.hy_skip[0:1, c0:c0 + NB].partition_broadcast(64), writes=[r_skb])
        P.dma("sp", skb[:, 1, :], io.hy_skip[1:2, c0:c0 + NB].partition_broadcast(64), writes=[r_skb])
        for n in range(2):
            zin, r_zin = zins.next()
            P.dma("sp", zin[:], sig_ap(io.hk, n * 512 + c0), reads=[G.r_hk], writes=[r_zin])
            fwd(zin, r_zin, filt_first(n), kp=128)
        P.dma("sp", tA[:], sig_ap(io.uT, c0), reads=[G.r_uT], writes=[r_A])
        P.dma("sp", tB[:], sig_ap(io.uT, 512 + c0), reads=[G.r_uT], writes=[r_B])
        zin, r_zin = zins.next()
        P.op("pool", lambda e, zin=zin: e.tensor_copy(zin[:64], tA[:]), reads=[r_A], writes=[r_zin])
        fwd(zin, r_zin, data_mul(0))
        P.op("pool", lambda e: e.tensor_tensor(out=tA[:], in0=tA[:], in1=skb[:, 0, :].unsqueeze(2).broadcast_to([64, NB, 128]), op=ALU.mult),
             reads=[r_A, r_skb], writes=[r_A])
        zin2, r_zin2 = zins.next()

        def gate0(g, ps, r_ps, zin2=zin2, r_zin2=r_zin2):
            bs = slice(g * 16, (g + 1) * 16)
            tm, r_tm = gtmp.next()
            P.op("dve", lambda e: e.tensor_tensor(out=tm[:], in0=ps[:].rearrange("p b c -> p c b"), in1=tA[:, :, bs], op=ALU.add),
                 reads=[r_ps, r_A], writes=[r_tm])
            P.op("pool", lambda e: e.tensor_tensor(out=tB[:, :, bs], in0=tm[:], in1=tB[:, :, bs], op=ALU.mult),
                 reads=[r_tm, r_B], writes=[r_B])
            P.op("act", lambda e: e.copy(out=zin2[:64, :, bs], in_=tB[:, :, bs]), reads=[r_B], writes=[r_zin2])
        inverse(gate0)
        P.dma("sp", tA[:], sig_ap(io.uT, 1024 + c0), reads=[G.r_uT], writes=[r_A])
        fwd(zin2, r_zin2, data_mul(1))
        P.op("pool", lambda e: e.tensor_tensor(out=tB[:], in0=tB[:], in1=skb[:, 1, :].unsqueeze(2).broadcast_to([64, NB, 128]), op=ALU.mult),
             reads=[r_B, r_skb], writes=[r_B])

        def gate1(g, ps, r_ps):
            bs = slice(g * 16, (g + 1) * 16)
            tm, r_tm = gtmp.next()
            P.op("dve", lambda e: e.tensor_tensor(out=tm[:], in0=ps[:].rearrange("p b c -> p c b"), in1=tB[:, :, bs], op=ALU.add),
                 reads=[r_ps, r_B], writes=[r_tm])
            P.op("pool", lambda e: e.tensor_tensor(out=tA[:, :, bs], in0=tm[:], in1=tA[:, :, bs], op=ALU.mult),
                 reads=[r_tm, r_A], writes=[r_A])
        inverse(gate1)
        P.dma("sp", sig_ap(io.z2T, c0), tA[:], reads=[r_A], writes=[G.r_z2T])


def phase_attn(P, io, G, nheads=4, nqc=NCH):
    NKT = NKEY // 128
    ones_b = P.sb("ones_b", [128, 128], BF16); ones_f = P.sb("ones_f", [128, 128], F32); r_c = P.res()
    P.op("pool", lambda e: e.memset(ones_b[:], 1.0), writes=[r_c])
    P.op("pool", lambda e: e.memset(ones_f[:], 1.0 / 128.0), writes=[r_c])
    lq = P.sb("lamq", [128, 4, 64], F32); r_lq = P.res()
    for i, a in enumerate((io.lam_q1, io.lam_k1, io.lam_q2, io.lam_k2)):
        P.dma("sp", lq[:, i, :], a.rearrange("(o d) -> o d", o=1).partition_broadcast(128), writes=[r_lq])
    lt = P.sb("lamt", [128, 2, 64], F32); ls = P.sb("lams", [128, 4], F32); r_lt = P.res()
    P.op("dve", lambda e: e.tensor_tensor(out=lt[:], in0=lq[:, 0:4:2, :], in1=lq[:, 1:4:2, :], op=ALU.mult), reads=[r_lq], writes=[r_lt])
    P.op("dve", lambda e: e.reduce_sum(out=ls[:, 0:2], in_=lt[:], axis=AX.X), reads=[r_lt], writes=[r_lt])
    P.op("act", lambda e: e.activation(out=ls[:, 0:2], in_=ls[:, 0:2], func=AF.Exp), reads=[r_lt], writes=[r_lt])
    P.op("dve", lambda e: e.tensor_tensor(out=ls[:, 2:3], in0=ls[:, 1:2], in1=ls[:, 0:1], op=ALU.subtract), reads=[r_lt], writes=[r_lt])
    P.op("dve", lambda e: e.tensor_scalar(out=ls[:, 3:4], in0=ls[:, 2:3], scalar1=-LAM_INIT, scalar2=None, op0=ALU.add), reads=[r_lt], writes=[r_lt])
    neglam = ls[:, 3:4]
    sg = P.sb("subg", [128, 1], F32); r_sg = P.res()
    P.dma("sp", sg[:], io.subln.rearrange("(p o) -> p o", o=1), writes=[r_sg])
    P.op("dve", lambda e: e.tensor_scalar(out=sg[:], in0=sg[:], scalar1=1.0 - LAM_INIT, scalar2=None, op0=ALU.mult), reads=[r_sg], writes=[r_sg])

    ones_1 = P.sb("ones_1", [128, 128], F32)
    P.op("pool", lambda e: e.memset(ones_1[:], 1.0), writes=[r_c])
    KT = Rot(P, "aKT", [128, NKEY], BF16, 2)
    VV = Rot(P, "aVV", [128, NKT, 128], BF16, 2)
    QT = Rot(P, "aQT", [128, 512], BF16, 3)
    pTs = Rot(P, "apT", [128, 512], BF16, 10)
    pairs = Rot(P, "apair", [128, 512], BF16, 4)
    stash = {}
    psS = Rot(P, "apsS", [128, 512], F32, 4, psum=True)
    psO = [P.ps(f"apsO{m}", [128, 512], F32) for m in range(2)]; r_O = [P.res(), P.res()]
    psX = Rot(P, "apsX", [128, 512], F32, 2, psum=True)
    accs = [Rot(P, f"aacc{m}", [128, 512], F32, 2) for m in range(2)]
    oraw = [Rot(P, f"aoraw{m}", [128, 512], F32, 2) for m in range(2)]
    rr = Rot(P, "arr", [128, 512], F32, 2)
    sqs = Rot(P, "asq", [128, 512], F32, 2)
    outs = Rot(P, "aout", [128, 512], BF16, 3)
    LA = 1
    steps = [(h, qc, kt) for h in range(nheads) for qc in range(nqc) for kt in range(NKT)]
    bufs = {}
    sbank = {}
    deferred = []
    cur_acc = {}

    def ensure_loaded(h, qc):
        if ("kv", h) not in bufs:
            kt_t, r_kt = KT.next(); vv_t, r_vv = VV.next()
            for s_ in range(4):
                cs = slice(s_ * (NKEY // 4), (s_ + 1) * (NKEY // 4))
                P.dma("sp" if s_ % 2 == 0 else "sp", kt_t[:, cs], io.kT[h * 128:(h + 1) * 128, cs], reads=[G.r_kT], writes=[r_kt])
            for s_ in range(6):
                ks = slice(s_ * 11, (s_ + 1) * 11)
                P.dma("sp" if s_ % 2 == 0 else "sp", vv_t[:, ks, :],
                      io.vv[s_ * 11 * 128:(s_ + 1) * 11 * 128, h * 128:(h + 1) * 128].rearrange("(k p) v -> p k v", p=128),
                      reads=[G.r_vv], writes=[r_vv])
            bufs[("kv", h)] = (kt_t, r_kt, vv_t, r_vv)
        if ("q", h, qc) not in bufs:
            q_t, r_q = QT.next()
            P.dma("sp", q_t[:], io.qT[h * 128:(h + 1) * 128, qc * 512:(qc + 1) * 512], reads=[G.r_qT], writes=[r_q])
            bufs[("q", h, qc)] = (q_t, r_q)

    def issue_qk(si):
        h, qc, kt = steps[si]
        ensure_loaded(h, qc)
        kt_t, r_kt, vv_t, r_vv = bufs[("kv", h)]
        q_t, r_q = bufs[("q", h, qc)]
        for m in range(2):
            ms = slice(m * 64, (m + 1) * 64)
            ps, r_ps = psS.next()
            sbank[(si, m)] = (ps, r_ps)
            P.op("pe", lambda e, ps=ps, ms=ms: e.matmul(ps[:], kt_t[ms, kt * 128:(kt + 1) * 128], q_t[ms, :], start=True, stop=True),
                 reads=[r_kt, r_q], writes=[r_ps])

    def epilogue1(h, qc, ac, orw):
        o_n = []
        for m in range(2):
            px, r_px = psX.next()
            P.op("pe", lambda e, px=px, m=m: e.matmul(px[:], ones_1[:], ac[m][0][:], start=True, stop=True),
                 reads=[r_c, ac[m][1]], writes=[r_px])
            r_t, r_r = rr.next()
            P.op("dve", lambda e, px=px, r_t=r_t: e.reciprocal(out=r_t[:], in_=px[:]), reads=[r_px], writes=[r_r])
            ot, r_ot = orw[m]
            P.op("dve", lambda e, ot=ot, r_t=r_t: e.tensor_tensor(out=ot[:], in0=ot[:], in1=r_t[:], op=ALU.mult), reads=[r_ot, r_r], writes=[r_ot])
            o_n.append((ot, r_ot))
        (o0, r_o0), (o1, r_o1) = o_n
        P.op("dve", lambda e: e.scalar_tensor_tensor(out=o0[:], in0=o1[:], scalar=neglam, in1=o0[:], op0=ALU.mult, op1=ALU.add),
             reads=[r_o1, r_o0, r_lt], writes=[r_o0])
        sq, r_sq = sqs.next()
        P.op("pool", lambda e: e.tensor_tensor(out=sq[:], in0=o0[:], in1=o0[:], op=ALU.mult), reads=[r_o0], writes=[r_sq])
        return o0, r_o0, sq, r_sq

    def epilogue2(h, qc, o0, r_o0, sq, r_sq):
        px, r_px = psX.next()
        P.op("pe", lambda e: e.matmul(px[:], ones_f[:], sq[:], start=True, stop=True), reads=[r_c, r_sq], writes=[r_px])
        rs, r_rs = rr.next()
        P.op("act", lambda e: e.activation(out=rs[:], in_=px[:], func=AF.Sqrt, bias=G.eps[:], scale=1.0),
             reads=[r_px, G.r_const], writes=[r_rs])
        P.op("dve", lambda e: e.reciprocal(out=rs[:], in_=rs[:]), reads=[r_rs], writes=[r_rs])
        P.op("dve", lambda e: e.tensor_tensor(out=o0[:], in0=o0[:], in1=rs[:], op=ALU.mult), reads=[r_rs, r_o0], writes=[r_o0])
        ob, r_ob = outs.next()
        P.op("dve", lambda e: e.tensor_scalar(out=ob[:], in0=o0[:], scalar1=sg[:, 0:1], scalar2=None, op0=ALU.mult),
             reads=[r_o0, r_sg], writes=[r_ob])
        P.dma("sp", io.mixT[h * 128:(h + 1) * 128, qc * 512:(qc + 1) * 512], ob[:], reads=[r_ob], writes=[G.r_mixT])

    def issue_rest(si):
        h, qc, kt = steps[si]
        kt_t, r_kt, vv_t, r_vv = bufs[("kv", h)]
        if kt == 0:
            cur_acc[0] = accs[0].next(); cur_acc[1] = accs[1].next()
        for m in range(2):
            ps, r_ps = sbank.pop((si, m))
            pT, r_pT = pTs.next()
            P.op("act", lambda e, ps=ps, pT=pT: e.activation(out=pT[:], in_=ps[:], func=AF.Exp, scale=0.125), reads=[r_ps], writes=[r_pT])
            P.op("pe", lambda e, pT=pT, m=m: e.matmul(psO[m][:], vv_t[:, kt, :], pT[:], start=(kt == 0), stop=(kt == NKT - 1)),
                 reads=[r_vv, r_pT], writes=[r_O[m]])
            ac, r_ac = cur_acc[m]
            if kt % 2 == 0:
                stash[m] = (pT, r_pT)
            else:
                p0, r_p0 = stash.pop(m)
                pr, r_pr = pairs.next()
                P.op("dve", lambda e, pr=pr, p0=p0, pT=pT: e.tensor_tensor(out=pr[:], in0=p0[:], in1=pT[:], op=ALU.add),
                     reads=[r_p0, r_pT], writes=[r_pr])
                if kt == 1:
                    P.op("dve", lambda e, ac=ac, pr=pr: e.tensor_copy(ac[:], pr[:]), reads=[r_pr], writes=[r_ac])
                else:
                    P.op("dve", lambda e, ac=ac, pr=pr: e.tensor_tensor(out=ac[:], in0=ac[:], in1=pr[:], op=ALU.add), reads=[r_pr, r_ac], writes=[r_ac])
        if kt == NKT - 1:
            orw = []
            for m in range(2):
                ot, r_ot = oraw[m].next()
                P.op("act", lambda e, ot=ot, m=m: e.copy(out=ot[:], in_=psO[m][:]), reads=[r_O[m]], writes=[r_ot])
                orw.append((ot, r_ot))
            ac = [cur_acc[0], cur_acc[1]]

            def d1(h=h, qc=qc, ac=ac, orw=orw, si=si):
                args = epilogue1(h, qc, ac, orw)
                deferred.append((si + 8, lambda: epilogue2(h, qc, *args)))
            deferred.append((si + 3, d1))

    n = len(steps)
    for si in range(min(LA, n)):
        issue_qk(si)
    for si in range(n):
        if si + LA < n:
            issue_qk(si + LA)
        issue_rest(si)
        while deferred and deferred[0][0] <= si:
            deferred.pop(0)[1]()
    while deferred:
        deferred.pop(0)[1]()


def load_weight_bf16(P, io_w, nk, ncols, dst, r_dst, scale_b=None, r_scale=None, name="wl", chunk=2048):
    st = Rot(P, name + "st", [128, chunk], F32, 2)
    n = 0
    for k in range(nk):
        for c0 in range(0, ncols, chunk):
            cw = min(chunk, ncols - c0)
            s, r_s = st.next()
            P.dma("sp" if n % 2 == 0 else "act", s[:, :cw], io_w[k * 128:(k + 1) * 128, c0:c0 + cw], writes=[r_s])
            eng = ("dve", "pool")[n % 2]
            if scale_b is None:
                P.op(eng, lambda e, s=s, k=k, c0=c0, cw=cw: e.tensor_copy(dst[:, k, c0:c0 + cw], s[:, :cw]), reads=[r_s], writes=[r_dst])
            else:
                P.op(eng, lambda e, s=s, k=k, c0=c0, cw=cw: e.tensor_tensor(out=dst[:, k, c0:c0 + cw], in0=s[:, :cw],
                                                                           in1=scale_b[:, c0:c0 + cw], op=ALU.mult),
                     reads=[r_s, r_scale], writes=[r_dst])
            n += 1


def phase_outproj(P, io, G, nchunks=NCH):
    gab = P.sb("gab", [128, D], F32); r_gab = P.res()
    P.dma("sp", gab[:], io.modv[0:1, 2 * D:3 * D].partition_broadcast(128), reads=[G.r_modv], writes=[r_gab])
    wo = P.sb("wo", [128, 8, D], BF16); r_wo = P.res()
    load_weight_bf16(P, io.w_out, 8, D, wo, r_wo, gab, r_gab, name="wo", chunk=1024)
    hn = P.sb("hyn", [128, 4], F32); r_hn = P.res()
    P.dma("sp", hn[:], io.hy_norm.rearrange("(j p) -> p j", p=128), writes=[r_hn], allow_slow_non_contiguous=True)
    ones_f = P.sb("ones_f2", [128, 128], F32); r_c = P.res()
    P.op("pool", lambda e: e.memset(ones_f[:], 1.0 / 512.0), writes=[r_c])
    z2s = Rot(P, "oz2", [128, 4, 512], F32, 2)
    sqs = Rot(P, "osq", [128, 4, 512], F32, 1)
    mix = Rot(P, "omix", [128, 8, 512], BF16, 2)
    rst = Rot(P, "orst", [128, 512], F32, 2)
    psM = Rot(P, "opsM", [128, 512], F32, 2, psum=True)
    psX = Rot(P, "opsX", [128, 512], F32, 4, psum=True)
    xts = Rot(P, "oxt", [128, D], F32, 8)
    def front(i):
        t0 = i * 512
        mx, r_mx = mix.next()
        P.dma("sp", mx[:, 0:4, :], io.mixT[:, t0:t0 + 512].rearrange("(k p) t -> p k t", p=128), reads=[G.r_mixT], writes=[r_mx])
        z2, r_z2 = z2s.next()
        P.dma("sp", z2[:], io.z2T[:, t0:t0 + 512].rearrange("(k p) t -> p k t", p=128), reads=[G.r_z2T], writes=[r_z2])
        sq, r_sq = sqs.next()
        P.op("dve", lambda e, sq=sq, z2=z2: e.tensor_tensor(out=sq[:], in0=z2[:], in1=z2[:], op=ALU.mult), reads=[r_z2], writes=[r_sq])
        pm, r_pm = psM.next()
        for k in range(4):
            P.op("pe", lambda e, pm=pm, sq=sq, k=k: e.matmul(pm[:], ones_f[:], sq[:, k, :], start=(k == 0), stop=(k == 3)),
                 reads=[r_c, r_sq], writes=[r_pm])
        rs, r_rs = rst.next()
        P.op("act", lambda e, pm=pm, rs=rs: e.activation(out=rs[:], in_=pm[:], func=AF.Sqrt, bias=G.eps[:], scale=1.0),
             reads=[r_pm, G.r_const], writes=[r_rs])
        P.op("dve", lambda e, rs=rs: e.reciprocal(out=rs[:], in_=rs[:]), reads=[r_rs], writes=[r_rs])
        for k in range(4):
            P.op("dve", lambda e, z2=z2, rs=rs, k=k: e.tensor_tensor(out=z2[:, k, :], in0=z2[:, k, :], in1=rs[:], op=ALU.mult),
                 reads=[r_z2, r_rs], writes=[r_z2])
            P.op("dve", lambda e, z2=z2, mx=mx, k=k: e.tensor_scalar(out=mx[:, 4 + k, :], in0=z2[:, k, :], scalar1=hn[:, k:k + 1], scalar2=None,
                                                                     op0=ALU.mult), reads=[r_z2, r_hn], writes=[r_mx])
        return mx, r_mx

    def back(i, mx, r_mx):
        t0 = i * 512
        for tt in range(4):
            xt, r_x = xts.next()
            P.dma("sp", xt[:], io.x[t0 + tt * 128:t0 + (tt + 1) * 128, :], writes=[r_x])
            for dh in range(2):
                px, r_px = psX.next()
                for k in range(8):
                    P.op("pe", lambda e, px=px, mx=mx, k=k, tt=tt, dh=dh: e.matmul(px[:], mx[:, k, tt * 128:(tt + 1) * 128],
                                                                                 wo[:, k, dh * 512:(dh + 1) * 512],
                                                                                 start=(k == 0), stop=(k == 7)),
                         reads=[r_mx, r_wo], writes=[r_px])
                P.op("dve", lambda e, px=px, xt=xt, dh=dh: e.tensor_tensor(out=xt[:, dh * 512:(dh + 1) * 512], in0=px[:],
                                                                        in1=xt[:, dh * 512:(dh + 1) * 512], op=ALU.add),
                     reads=[r_px, r_x], writes=[r_x])
            P.dma("sp", io.x2[t0 + tt * 128:t0 + (tt + 1) * 128, :], xt[:], reads=[r_x], writes=[G.r_x2])

    nxt = front(0)
    for i in range(nchunks):
        cur = nxt
        if i + 1 < nchunks:
            nxt = front(i + 1)
        back(i, *cur)


def phase_ffn(P, io, G, nchunks=NCH):
    NJ = DFF // 128
    fnb = P.sb("fnb", [128, D], F32); r_fnb = P.res()
    P.dma("sp", fnb[:], io.final_norm.rearrange("(o d) -> o d", o=1).partition_broadcast(128), writes=[r_fnb])
    wup = P.sb("wup", [128, 8, 2 * DFF], BF16); r_wup = P.res()
    wdn = P.sb("wdn", [128, NJ, D], BF16); r_wdn = P.res()
    cw = P.sb("fcw", [128, NJ, 4], F32); r_cw = P.res()
    for i in range(3):
        P.dma("sp", cw[:, :, i], io.ffn_conv_w[i].rearrange("(j p) -> p j", p=128), writes=[r_cw], allow_slow_non_contiguous=True)
    P.dma("sp", cw[:, :, 3], io.ffn_conv_b.rearrange("(j p) -> p j", p=128), writes=[r_cw], allow_slow_non_contiguous=True)
    with P.temp_scope():
        gfb = P.sb("gfb", [128, D], F32); r_gfb = P.res()
        P.dma("sp", gfb[:], io.modv[0:1, 5 * D:6 * D].partition_broadcast(128), reads=[G.r_modv], writes=[r_gfb])
        load_weight_bf16(P, io.ffn_w_up, 8, 2 * DFF, wup, r_wup, name="wu", chunk=1408)
        load_weight_bf16(P, io.ffn_w_down, NJ, D, wdn, r_wdn, gfb, r_gfb, name="wd", chunk=1024)

    T = {"junk": Rot(P, "fjunk", [128, D], BF16, 1), "stat": Rot(P, "fstat", [128, 4], F32, 4)}
    xts = Rot(P, "fxt", [128, D], F32, 1)
    xns = Rot(P, "fxn", [128, D], BF16, 2)
    hfs = Rot(P, "fhf", [128, 8, 514], BF16, 2)
    halo = Rot(P, "fhalo", [2, D], F32, 1)
    halon = Rot(P, "fhalon", [2, D], BF16, 1)
    pTs = Rot(P, "fpT", [128, 8, 128], BF16, 1, psum=True)
    psA = Rot(P, "fpsA", [128, 512], F32, 2, psum=True)
    psB = Rot(P, "fpsB", [128, 512], F32, 2, psum=True)
    psH = Rot(P, "fpsH", [128, 2], F32, 1, psum=True)
    psD = Rot(P, "fpsD", [128, 512], F32, 2, psum=True)
    exts = Rot(P, "fext", [128, 514], F32, 1)
    acs = Rot(P, "fac", [128, 512], F32, 2)
    gT = P.sb("fgT", [128, NJ, 512], BF16); r_gT = P.res()
    outs = Rot(P, "fout", [128, D], F32, 2)
    flip = [0]

    def tr_mod(xn, r_xn, npart, hx, r_hx, cols):
        pT, r_pT = pTs.next()
        for k in range(8):
            P.op("pe", lambda e, k=k, pT=pT: e.transpose(pT[:, k, 0:npart], xn[:npart, k * 128:(k + 1) * 128], G.ident_b[:npart, :npart]),
                 reads=[r_xn, G.r_const], writes=[r_pT])
        for k in range(8):
            flip[0] ^= 1
            if flip[0] and npart == 128:
                P.op("act", lambda e, k=k, pT=pT: e.activation(out=hx[:, k, cols], in_=pT[:, k, 0:npart], func=AF.Identity,
                                                              scale=G.gainF[:, 0, k:k + 1], bias=G.mods[:, 0, 24 + k:25 + k]),
                     reads=[r_pT, G.r_gain, G.r_mods], writes=[r_hx])
            else:
                P.op("dve", lambda e, k=k, pT=pT: e.tensor_scalar(out=hx[:, k, cols], in0=pT[:, k, 0:npart], scalar1=G.gainF[:, 0, k:k + 1],
                                                                 scalar2=G.mods[:, 0, 24 + k:25 + k], op0=ALU.mult, op1=ALU.add),
                     reads=[r_pT, G.r_gain, G.r_mods], writes=[r_hx])

    def front(i):
        t0 = i * 512
        hx, r_hx = hfs.next()
        hl, r_hl = halo.next()
        if i == 0 or i == NCH - 1:
            P.op("pool", lambda e, hl=hl: e.memset(hl[0:2, :], 0.0), writes=[r_hl])
        if i != 0:
            P.dma("sp", hl[0:1, :], io.x2[t0 - 1:t0, :], reads=[G.r_x2], writes=[r_hl])
        if i != NCH - 1:
            P.dma("sp", hl[1:2, :], io.x2[t0 + 512:t0 + 513, :], reads=[G.r_x2], writes=[r_hl])
        hn, r_hn = halon.next()
        norm_tile(P, G, hl, r_hl, 2, hn, r_hn, T)
        tr_mod(hn, r_hn, 2, hx, r_hx, slice(0, 514, 513))
        if i == 0:
            P.op("dve", lambda e, hx=hx: e.memset(hx[:, :, 0:1], 0.0), writes=[r_hx])
        if i == NCH - 1:
            P.op("dve", lambda e, hx=hx: e.memset(hx[:, :, 513:514], 0.0), writes=[r_hx])
        for tt in range(4):
            xt, r_x = xts.next()
            P.dma("sp", xt[:], io.x2[t0 + tt * 128:t0 + (tt + 1) * 128, :], reads=[G.r_x2], writes=[r_x])
            xn, r_xn = xns.next()
            norm_tile(P, G, xt, r_x, 128, xn, r_xn, T)
            tr_mod(xn, r_xn, 128, hx, r_hx, slice(1 + tt * 128, 1 + (tt + 1) * 128))
        return hx, r_hx

    def back(i, hx, r_hx):
        t0 = i * 512
        for j in range(NJ):
            pa, r_pa = psA.next(); ph, r_ph = psH.next(); pb, r_pb = psB.next()
            for k in range(8):
                P.op("pe", lambda e, k=k, j=j, pa=pa, hx=hx: e.matmul(pa[:], wup[:, k, j * 128:(j + 1) * 128], hx[:, k, 1:513],
                                                                     start=(k == 0), stop=(k == 7)), reads=[r_wup, r_hx], writes=[r_pa])
            for k in range(8):
                P.op("pe", lambda e, k=k, j=j, ph=ph, hx=hx: e.matmul(ph[:], wup[:, k, j * 128:(j + 1) * 128], hx[:, k, 0:514:513],
                                                                     start=(k == 0), stop=(k == 7)), reads=[r_wup, r_hx], writes=[r_ph])
            for k in range(8):
                P.op("pe", lambda e, k=k, j=j, pb=pb, hx=hx: e.matmul(pb[:], wup[:, k, DFF + j * 128:DFF + (j + 1) * 128], hx[:, k, 1:513],
                                                                     start=(k == 0), stop=(k == 7)), reads=[r_wup, r_hx], writes=[r_pb])
            ex, r_ex = exts.next()
            P.op("act", lambda e, pa=pa, ex=ex: e.copy(out=ex[:, 1:513], in_=pa[:]), reads=[r_pa], writes=[r_ex])
            P.op("act", lambda e, ph=ph, ex=ex: e.copy(out=ex[:, 0:514:513], in_=ph[:]), reads=[r_ph], writes=[r_ex])
            ac, r_ac = acs.next()
            P.op("dve", lambda e, ex=ex, ac=ac, j=j: e.tensor_scalar(out=ac[:], in0=ex[:, 1:513], scalar1=cw[:, j, 1:2], scalar2=cw[:, j, 3:4],
                                                                    op0=ALU.mult, op1=ALU.add), reads=[r_ex, r_cw], writes=[r_ac])
            P.op("dve", lambda e, ex=ex, ac=ac, j=j: e.scalar_tensor_tensor(out=ac[:], in0=ex[:, 0:512], scalar=cw[:, j, 0:1], in1=ac[:],
                                                                           op0=ALU.mult, op1=ALU.add), reads=[r_ex, r_cw, r_ac], writes=[r_ac])
            P.op("dve", lambda e, ex=ex, ac=ac, j=j: e.scalar_tensor_tensor(out=ac[:], in0=ex[:, 2:514], scalar=cw[:, j, 2:3], in1=ac[:],
                                                                            op0=ALU.mult, op1=ALU.add), reads=[r_ex, r_cw, r_ac], writes=[r_ac])
            P.op("act", lambda e, ac=ac: e.activation(out=ac[:], in_=ac[:], func=AF.Gelu), reads=[r_ac], writes=[r_ac])
            P.op("dve", lambda e, ac=ac, pb=pb, j=j: e.tensor_tensor(out=gT[:, j, :], in0=pb[:], in1=ac[:], op=ALU.mult),
                 reads=[r_pb, r_ac], writes=[r_gT])
        for tt in range(4):
            ot, r_ot = outs.next()
            P.dma("sp", ot[:], io.x2[t0 + tt * 128:t0 + (tt + 1) * 128, :], reads=[G.r_x2], writes=[r_ot])
            for dh in range(2):
                pd, r_pd = psD.next()
                for j in range(NJ):
                    P.op("pe", lambda e, pd=pd, j=j, tt=tt, dh=dh: e.matmul(pd[:], gT[:, j, tt * 128:(tt + 1) * 128],
                                                                          wdn[:, j, dh * 512:(dh + 1) * 512],
                                                                          start=(j == 0), stop=(j == NJ - 1)),
                         reads=[r_gT, r_wdn], writes=[r_pd])
                P.op("dve", lambda e, pd=pd, ot=ot, dh=dh: e.tensor_tensor(out=ot[:, dh * 512:(dh + 1) * 512], in0=pd[:],
                                                                        in1=ot[:, dh * 512:(dh + 1) * 512], op=ALU.add),
                     reads=[r_pd, r_ot], writes=[r_ot])
            junk, r_junk = T["junk"].next(); st, r_st = T["stat"].next()
            P.op("act", lambda e, junk=junk, ot=ot, st=st: e.activation(out=junk[:], in_=ot[:], func=AF.Square, accum_out=st[:, 0:1]),
                 reads=[r_ot], writes=[r_junk, r_st])
            P.op("act", lambda e, st=st: e.activation(out=st[:, 1:2], in_=st[:, 0:1], func=AF.Sqrt, scale=1.0 / D, bias=G.eps[:]),
                 reads=[r_st, G.r_const], writes=[r_st])
            P.op("dve", lambda e, st=st: e.reciprocal(out=st[:, 2:3], in_=st[:, 1:2]), reads=[r_st], writes=[r_st])
            P.op("dve", lambda e, ot=ot, st=st: e.scalar_tensor_tensor(out=ot[:], in0=ot[:], scalar=st[:, 2:3], in1=fnb[:],
                                                                       op0=ALU.mult, op1=ALU.mult), reads=[r_ot, r_st, r_fnb], writes=[r_ot])
            P.dma("sp", io.out[t0 + tt * 128:t0 + (tt + 1) * 128, :], ot[:], reads=[r_ot], writes=[G.r_out])

    nxt = front(0)
    for i in range(nchunks):
        cur = nxt
        if i + 1 < nchunks:
            nxt = front(i + 1)
        back(i, *cur)
```
